# Optimizing a Trainium2 kernel written in Bass

```python
import math
import jax, jax.numpy as jnp
from jax import lax
import numpy as np

D_MODEL = 2048
BATCH = 8
SEQ = 4096
DEPTH = 2

N_A_LAYERS = max(DEPTH // 2, 1)
N_B_LAYERS = DEPTH - N_A_LAYERS

SSM_EXPAND = 2
D_INNER = SSM_EXPAND * D_MODEL
SSM_HEAD_DIM = 64
SSM_HEADS = D_INNER // SSM_HEAD_DIM
SSM_GROUPS = 8
D_STATE = 128
D_CONV = 4
CHUNK = 128
GN = SSM_GROUPS * D_STATE
CONV_DIM = D_INNER + 2 * GN
D_IN_PROJ = D_INNER + CONV_DIM + SSM_HEADS

ATT_HEAD_DIM = 64
ATT_HEADS = D_MODEL // ATT_HEAD_DIM
KV_HEADS = 4
Q_PER_KV = ATT_HEADS // KV_HEADS
Q_WIDTH = ATT_HEADS * ATT_HEAD_DIM
KV_WIDTH = KV_HEADS * ATT_HEAD_DIM
WINDOW = 128
ROT_DIM = ATT_HEAD_DIM // 4
ROPE_THETA = 500000.0

ALPHA = (2.0 * DEPTH) ** 0.25
BETA = (8.0 * DEPTH) ** -0.25
EPS = 1e-5

kernel_name = "yoco_mamba2_swa_sink_hybrid"


def _layernorm(x, g, b):
    xf = x.astype(jnp.float32)
    mu = jnp.mean(xf, axis=-1, keepdims=True)
    var = jnp.mean(jnp.square(xf - mu), axis=-1, keepdims=True)
    y = (xf - mu) * lax.rsqrt(var + EPS) * g.astype(jnp.float32) + b.astype(jnp.float32)
    return y.astype(x.dtype)


def _segsum(x):
    t = x.shape[-1]
    xe = jnp.broadcast_to(x[..., :, None], x.shape + (t,))
    xe = jnp.where(jnp.tril(jnp.ones((t, t), dtype=bool), -1), xe, 0.0)
    s = jnp.cumsum(xe, axis=-2)
    return jnp.where(jnp.tril(jnp.ones((t, t), dtype=bool)), s, -jnp.inf)


def _ssd_chunked(xs, dt, a, bm, cm):
    bsz, seqlen, nh, hp = xs.shape
    ng = bm.shape[2]
    rg = nh // ng
    nc = seqlen // CHUNK
    x_c = (xs * dt[..., None]).reshape(bsz, nc, CHUNK, ng, rg, hp)
    b_c = bm.reshape(bsz, nc, CHUNK, ng, D_STATE)
    c_c = cm.reshape(bsz, nc, CHUNK, ng, D_STATE)
    da = (dt * a).reshape(bsz, nc, CHUNK, ng, rg).transpose(0, 3, 4, 1, 2)
    da_cum = jnp.cumsum(da, axis=-1)
    decay_in = jnp.exp(_segsum(da))
    cb = jnp.einsum("bcqgn,bcsgn->bgcqs", c_c, b_c)
    y_diag = jnp.einsum("bgcqs,bgrcqs,bcsgrp->bcqgrp", cb, decay_in, x_c)
    decay_st = jnp.exp(da_cum[..., -1:] - da_cum)
    states = jnp.einsum("bcsgn,bgrcs,bcsgrp->bcgrpn", b_c, decay_st, x_c)
    chunk_tot = jnp.pad(da_cum[..., -1], ((0, 0), (0, 0), (0, 0), (1, 0)))
    decay_ch = jnp.exp(_segsum(chunk_tot))
    states0 = jnp.concatenate([jnp.zeros_like(states[:, :1]), states], axis=1)
    start_states = jnp.einsum("bgrzc,bcgrpn->bzgrpn", decay_ch, states0)[:, :-1]
    y_off = jnp.einsum("bcqgn,bcgrpn,bgrcq->bcqgrp", c_c, start_states, jnp.exp(da_cum))
    return (y_diag + y_off).reshape(bsz, seqlen, nh, hp)


def _mamba2_mixer(x, w_in, conv_w, conv_b, dt_bias, a_log, d_skip, norm_w, w_out):
    bsz, seqlen, _ = x.shape
    zxbcdt = x @ w_in
    z = zxbcdt[..., :D_INNER]
    xbc = zxbcdt[..., D_INNER:D_INNER + CONV_DIM]
    dt_raw = zxbcdt[..., D_INNER + CONV_DIM:]
    xbc = lax.conv_general_dilated(
        xbc, conv_w[:, None, :].astype(xbc.dtype), window_strides=(1,),
        padding=[(D_CONV - 1, 0)], dimension_numbers=("NWC", "WIO", "NWC"),
        feature_group_count=CONV_DIM) + conv_b
    xbc = jax.nn.silu(xbc)
    xs = xbc[..., :D_INNER].reshape(bsz, seqlen, SSM_HEADS, SSM_HEAD_DIM).astype(jnp.float32)
    bm = xbc[..., D_INNER:D_INNER + GN].reshape(bsz, seqlen, SSM_GROUPS, D_STATE).astype(jnp.float32)
    cm = xbc[..., D_INNER + GN:].reshape(bsz, seqlen, SSM_GROUPS, D_STATE).astype(jnp.float32)
    dt = jax.nn.softplus(dt_raw.astype(jnp.float32) + dt_bias.astype(jnp.float32))
    a = -jnp.exp(a_log.astype(jnp.float32))
    y = _ssd_chunked(xs, dt, a, bm, cm) + xs * d_skip.astype(jnp.float32)[:, None]
    y = y.reshape(bsz, seqlen, D_INNER) * jax.nn.silu(z.astype(jnp.float32))
    yg = y.reshape(bsz, seqlen, SSM_GROUPS, D_INNER // SSM_GROUPS)
    yg = yg * lax.rsqrt(jnp.mean(jnp.square(yg), axis=-1, keepdims=True) + EPS)
    y = yg.reshape(bsz, seqlen, D_INNER) * norm_w.astype(jnp.float32)
    return y.astype(x.dtype) @ w_out


def _rope_tables(positions):
    inv_freq = ROPE_THETA ** (-jnp.arange(0, ROT_DIM, 2, dtype=jnp.float32) / ROT_DIM)
    ang = positions.astype(jnp.float32)[..., None] * inv_freq
    return jnp.cos(ang), jnp.sin(ang)


def _rope_partial(x, cos, sin):
    half = ROT_DIM // 2
    xf = x.astype(jnp.float32)
    x1, x2, rest = xf[..., :half], xf[..., half:ROT_DIM], xf[..., ROT_DIM:]
    out = jnp.concatenate([x1 * cos - x2 * sin, x2 * cos + x1 * sin, rest], axis=-1)
    return out.astype(x.dtype)


def _shared_kv(x, kv_w, kv_b, cos, sin):
    bsz, seqlen, _ = x.shape
    kv = x @ kv_w + kv_b
    k = kv[..., :KV_WIDTH].reshape(bsz, seqlen, KV_HEADS, ATT_HEAD_DIM)
    v = kv[..., KV_WIDTH:].reshape(bsz, seqlen, KV_HEADS, ATT_HEAD_DIM)
    k = _rope_partial(k, cos[:, :, None, :], sin[:, :, None, :])
    return k, v


def _swa_sink_attention(q, k, v, sinks):
    bsz, seqlen, hk, rq, hd = q.shape
    nb = seqlen // WINDOW
    qb = q.reshape(bsz, nb, WINDOW, hk, rq, hd)
    kb = k.reshape(bsz, nb, WINDOW, hk, hd)
    vb = v.reshape(bsz, nb, WINDOW, hk, hd)
    pad = ((0, 0), (1, 0), (0, 0), (0, 0), (0, 0))
    kk = jnp.concatenate([jnp.pad(kb, pad)[:, :-1], kb], axis=2)
    vv = jnp.concatenate([jnp.pad(vb, pad)[:, :-1], vb], axis=2)
    s = jnp.einsum("bcqhrd,bckhd->bchrqk", qb, kk).astype(jnp.float32) * (hd ** -0.5)
    qi = jnp.arange(WINDOW)[:, None] + WINDOW
    ki = jnp.arange(2 * WINDOW)[None, :]
    diff = qi - ki
    kpos = jnp.arange(nb)[:, None, None] * WINDOW + ki[None] - WINDOW
    valid = (diff >= 0)[None] & (diff < WINDOW)[None] & (kpos >= 0)
    s = jnp.where(valid[None, :, None, None], s, -jnp.inf)
    sk = sinks.astype(jnp.float32)[None, None, :, :, None, None]
    m = jnp.maximum(jnp.max(s, axis=-1, keepdims=True), sk)
    p = jnp.exp(s - m)
    p = p / (jnp.sum(p, axis=-1, keepdims=True) + jnp.exp(sk - m))
    o = jnp.einsum("bchrqk,bckhd->bcqhrd", p.astype(vv.dtype), vv)
    return o.reshape(bsz, seqlen, hk * rq * hd)


def _swa_mixer(x, k, v, cos, sin, w_in, q_bias, sinks, w_out):
    bsz, seqlen, _ = x.shape
    proj = x @ w_in
    q = (proj[..., :Q_WIDTH] + q_bias).reshape(bsz, seqlen, KV_HEADS, Q_PER_KV, ATT_HEAD_DIM)
    gate = proj[..., Q_WIDTH:]
    q = _rope_partial(q, cos[:, :, None, None, :], sin[:, :, None, None, :])
    o = _swa_sink_attention(q, k, v, sinks.reshape(KV_HEADS, Q_PER_KV))
    o = o * jax.nn.silu(gate)
    return o @ w_out


def setup_inputs(seed: int = 0) -> dict:
    key = jax.random.key(seed)
    ks = jax.random.split(key, 20)
    f32 = jnp.float32
    x = jax.random.normal(ks[0], (BATCH, SEQ, D_MODEL), f32)
    offs = jax.random.randint(ks[1], (BATCH, 1), 0, 1024, dtype=jnp.int32)
    positions = (offs + jnp.arange(SEQ, dtype=jnp.int32)[None, :]).astype(jnp.int32)
    ln_g = 1.0 + 0.02 * jax.random.normal(ks[2], (DEPTH, D_MODEL), f32)
    ln_b = 0.02 * jax.random.normal(ks[3], (DEPTH, D_MODEL), f32)
    a_w_in = jax.random.normal(ks[4], (N_A_LAYERS, D_MODEL, D_IN_PROJ), f32) * D_MODEL ** -0.5
    a_conv_w = jax.random.normal(ks[5], (N_A_LAYERS, D_CONV, CONV_DIM), f32) * D_CONV ** -0.5
    a_conv_b = 0.02 * jax.random.normal(ks[6], (N_A_LAYERS, CONV_DIM), f32)
    dt0 = jnp.exp(jax.random.uniform(ks[7], (N_A_LAYERS, SSM_HEADS), f32,
                                     math.log(1e-3), math.log(1e-1)))
    a_dt_bias = dt0 + jnp.log(-jnp.expm1(-dt0))
    a_log = jnp.log(jax.random.uniform(ks[8], (N_A_LAYERS, SSM_HEADS), f32, 1.0, 16.0))
    a_d = 1.0 + 0.1 * jax.random.normal(ks[9], (N_A_LAYERS, SSM_HEADS), f32)
    a_norm_w = 1.0 + 0.02 * jax.random.normal(ks[10], (N_A_LAYERS, D_INNER), f32)
    a_w_out = jax.random.normal(ks[11], (N_A_LAYERS, D_INNER, D_MODEL), f32) * (D_INNER ** -0.5 * BETA)
    kv_scale = jnp.concatenate([jnp.ones((KV_WIDTH,), f32), jnp.full((KV_WIDTH,), BETA, f32)])
    kv_w = jax.random.normal(ks[12], (D_MODEL, 2 * KV_WIDTH), f32) * D_MODEL ** -0.5 * kv_scale
    kv_b = 0.02 * jax.random.normal(ks[13], (2 * KV_WIDTH,), f32)
    b_w_in = jax.random.normal(ks[14], (N_B_LAYERS, D_MODEL, 2 * Q_WIDTH), f32) * D_MODEL ** -0.5
    b_q_bias = 0.02 * jax.random.normal(ks[15], (N_B_LAYERS, Q_WIDTH), f32)
    b_sinks = 0.5 * jax.random.normal(ks[16], (N_B_LAYERS, ATT_HEADS), f32)
    b_w_out = jax.random.normal(ks[17], (N_B_LAYERS, Q_WIDTH, D_MODEL), f32) * (Q_WIDTH ** -0.5 * BETA)
    return {"x": x, "positions": positions, "ln_g": ln_g, "ln_b": ln_b,
            "a_w_in": a_w_in, "a_conv_w": a_conv_w, "a_conv_b": a_conv_b,
            "a_dt_bias": a_dt_bias, "a_log": a_log, "a_d": a_d, "a_norm_w": a_norm_w,
            "a_w_out": a_w_out, "kv_w": kv_w, "kv_b": kv_b, "b_w_in": b_w_in,
            "b_q_bias": b_q_bias, "b_sinks": b_sinks, "b_w_out": b_w_out}


def reference(x, positions, ln_g, ln_b, a_w_in, a_conv_w, a_conv_b, a_dt_bias, a_log, a_d,
              a_norm_w, a_w_out, kv_w, kv_b, b_w_in, b_q_bias, b_sinks, b_w_out):
    cos, sin = _rope_tables(positions)
    k_sh = None
    v_sh = None
    for layer in range(DEPTH):
        if layer < N_A_LAYERS:
            i = layer
            h = _mamba2_mixer(x, a_w_in[i], a_conv_w[i], a_conv_b[i], a_dt_bias[i], a_log[i],
                              a_d[i], a_norm_w[i], a_w_out[i])
        else:
            if layer == N_A_LAYERS:
                k_sh, v_sh = _shared_kv(x, kv_w, kv_b, cos, sin)
            j = layer - N_A_LAYERS
            h = _swa_mixer(x, k_sh, v_sh, cos, sin, b_w_in[j], b_q_bias[j], b_sinks[j], b_w_out[j])
        x = _layernorm(ALPHA * x + h, ln_g[layer], ln_b[layer])
    return x
```

```python
import math
from contextlib import ExitStack
import numpy as np
import concourse.bass as bass
import concourse.mybir as mybir
from concourse.ap import AP
from concourse.bass_utils import run_bass_kernel_spmd

F32 = mybir.dt.float32
BF = mybir.dt.bfloat16
I32 = mybir.dt.int32
AF = mybir.ActivationFunctionType
ALU = mybir.AluOpType

D = 2048
SEQ = 4096
T = 512
DIN = 4096
NPROJ = 10304
ALPHA = (2.0 * 2) ** 0.25
EPS = 1e-5
NEG = -30000.0

C_ID, C_U, C_ONES, C_NEGM, C_NEGMP, C_PMAT, C_ONE_E, C_ONE_O, C_FREQ, C_SGN = 0, 128, 256, 384, 512, 640, 768, 896, 1024, 1025
NCST = 1028
P_CW, P_CB, P_DTB, P_ALOG, P_AD, P_NW, P_KB, P_VB, P_QB, P_SINK = 0, 192, 240, 304, 368, 432, 464, 468, 724, 740
NPAR = 756

SL_Z, SL_XS, SL_B, SL_C, SL_WO, SL_K, SL_Q, SL_G, SL_O2 = 0, 8, 16, 18, 20, 28, 29, 33, 37
NSLAB = 41


class Buf:
    __slots__ = ("name", "w", "r", "excl")

    def __init__(self, name="", excl=False):
        self.name = name
        self.w = None
        self.r = []
        self.excl = excl


class _Rec:
    def __getattr__(self, name):
        def f(*a, **k):
            self.call = (name, a, k)
            return self
        return f


class Sched:
    CH = 2000
    ENGS = ("pe", "act", "dve", "pool", "sp")
    SAME_SYNC = {"pe": False, "act": True, "dve": True, "pool": True, "sp": False}

    def __init__(self, nc, es, ring=20):
        self.nc = nc
        self.es = es
        self.q = {e: [] for e in self.ENGS}
        self.cnt = {e: 0 for e in self.ENGS}
        self.sem = {e: None for e in self.ENGS}
        self.waited = {e: {} for e in self.ENGS}
        self.nsem = 0
        self.ring = {e: [] for e in ("sp", "pool", "act")}
        self.ringn = {e: 0 for e in ("sp", "pool", "act")}
        self.ringsz = ring
        self.out_tokens = []

    def _newsem(self, name):
        self.nsem += 1
        return self.es.enter_context(self.nc.semaphore(f"{name}{self.nsem}"))

    def _need(self, eng, tok, waits):
        if tok is None:
            return
        src, sem, val = tok
        if src == eng and not self.SAME_SYNC[eng]:
            return
        key = id(sem)
        if self.waited[eng].get(key, 0) >= val:
            return
        self.waited[eng][key] = val
        waits.append((sem, val))

    def _deps(self, eng, reads, writes):
        waits = []
        for b in reads:
            self._need(eng, b.w, waits)
            if b.excl:
                for t in b.r:
                    if t[0] != eng:
                        self._need(eng, t, waits)
        for b in writes:
            self._need(eng, b.w, waits)
            for t in b.r:
                self._need(eng, t, waits)
        return waits

    def op(self, eng, fn, reads=(), writes=(), signal=True):
        rec = _Rec()
        fn(rec)
        fn = rec.call
        if self.sem[eng] is None or self.cnt[eng] >= self.CH:
            self.sem[eng] = self._newsem("e_" + eng)
            self.cnt[eng] = 0
        tok = (eng, self.sem[eng], self.cnt[eng] + 1)
        waits = self._deps(eng, reads, writes)
        if signal:
            self.cnt[eng] += 1
        self.q[eng].append((waits, fn, self.sem[eng] if signal else None, 1))
        for b in reads:
            b.r.append(tok)
        for b in writes:
            b.w = tok
            b.r = []
        return tok

    def dma(self, eng, out, in_, reads=(), writes=(), is_output=False, **kw):
        r = self.ring[eng]
        i = self.ringn[eng] % self.ringsz
        self.ringn[eng] += 1
        if i >= len(r):
            r.append([self._newsem("d_" + eng), 0])
        sem, k = r[i]
        waits = self._deps(eng, reads, writes)
        if k > 0:
            self._need(eng, (None, sem, 16 * k), waits)
        r[i][1] = k + 1
        tok = (None, sem, 16 * (k + 1))
        self.q[eng].append((waits, lambda e: e.dma_start(out=out, in_=in_, **kw), sem, 16))
        for b in reads:
            b.r.append(tok)
        for b in writes:
            b.w = tok
            b.r = []
        if is_output:
            self.out_tokens.append(tok)
        return tok

    def fence(self, engs=("pe", "act", "dve", "pool")):
        toks = []
        for e in engs:
            if self.sem[e] is None or self.cnt[e] == 0:
                continue
            toks.append((e, self.sem[e], self.cnt[e]))
        for e in engs:
            waits = []
            for t in toks:
                if t[0] != e:
                    self._need(e, t, waits)
            for sem, k in self.ring["pool"]:
                if k > 0:
                    self._need(e, (None, sem, 16 * k), waits)
            if waits:
                self.q[e].append((waits, None, None, 0))

    def emit(self, block):
        nc = self.nc
        fin = []
        for t in self.out_tokens:
            self._need("pool", t, fin)
        if fin:
            self.q["pool"].append((fin, None, None, 0))

        def run(eng_name):
            def body(e):
                for waits, fn, sem, inc in self.q[eng_name]:
                    for (s, v) in waits:
                        e.wait_ge(s, v)
                    if fn is not None:
                        if callable(fn):
                            ins = fn(e)
                        else:
                            ins = getattr(e, fn[0])(*fn[1], **fn[2])
                        if sem is not None:
                            ins.then_inc(sem, inc)
            return body
        block.tensor(run("pe"))
        block.scalar(run("act"))
        block.vector(run("dve"))
        block.gpsimd(run("pool"))
        block.sync(run("sp"))


def bc(ap, n):
    return AP(ap.tensor, ap.offset, [list(x) for x in ap.ap] + [[0, n]])


class _Stop(Exception):
    pass


def build(NT=8, dbg=False, stop_after=None):
    def stage(name):
        if stop_after == name:
            raise _Stop()
    nc = bass.Bass("TRN2", target_bir_lowering=False)
    x_d = nc.dram_tensor("x", [SEQ, D], F32, kind="ExternalInput").ap()
    pos_d = nc.dram_tensor("pos", [1, SEQ], I32, kind="ExternalInput").ap()
    win_d = nc.dram_tensor("w_in", [D, NPROJ], F32, kind="ExternalInput").ap()
    wout_d = nc.dram_tensor("w_out", [DIN, D], F32, kind="ExternalInput").ap()
    kvw_d = nc.dram_tensor("kv_w", [D, 512], F32, kind="ExternalInput").ap()
    bwin_d = nc.dram_tensor("bw_in", [D, 4096], F32, kind="ExternalInput").ap()
    bwout_d = nc.dram_tensor("bw_out", [D, D], F32, kind="ExternalInput").ap()
    par_d = nc.dram_tensor("par", [128, NPAR], F32, kind="ExternalInput").ap()
    cst_d = nc.dram_tensor("cst", [128, NCST], F32, kind="ExternalInput").ap()
    lng_d = nc.dram_tensor("lng", [4, 128, D], F32, kind="ExternalInput").ap()
    out_d = nc.dram_tensor("out", [SEQ, D], F32, kind="ExternalOutput").ap()
    if dbg:
        dbg_d = nc.dram_tensor("dbg", [NT * T, D], F32, kind="ExternalOutput").ap()
    wb_d = nc.dram_tensor("wb", [NSLAB, 128, 16, 512], BF).ap()
    wbdt_d = nc.dram_tensor("wbdt", [128, 16, 64], BF).ap()
    wbv_d = nc.dram_tensor("wbv", [128, 16, 256], BF).ap()

    es = ExitStack()
    with es:
        def sb(name, shape, dt):
            return es.enter_context(nc.sbuf_tensor(name, shape, dt))
        S = Sched(nc, es)
        cst = sb("cst_sb", [128, NCST], F32)
        par = sb("par_sb", [128, NPAR], F32)
        identb = sb("identb", [128, 128], BF)
        negm4 = sb("negm4", [128, 512], BF)
        negmp4 = sb("negmp4", [128, 512], BF)
        pmatb = sb("pmatb", [128, 128], BF)
        oneE = sb("oneE", [128, 128], BF)
        oneO = sb("oneO", [128, 128], BF)
        Abc = sb("Abc", [128, 64], F32)
        expsink = sb("expsink", [128, 16], F32)
        lng = sb("lng_sb", [128, 2, D], F32)
        NSLOT = 2
        wslot = [sb(f"wslot{i}", [128, 16, 512], BF) for i in range(NSLOT)]
        wdt = sb("wdt", [128, 16, 64], BF)
        xT = sb("xT", [128, 16, T], BF)
        xin = sb("xin", [128, D], F32)
        Sst = sb("Sst", [128, 8, 512], F32)
        Sbf4 = [sb(f"Sbf{i}", [128, 512], BF) for i in range(4)]
        halo = sb("halo", [128, 48, 3], F32)
        Kcar = sb("Kcar", [128, 4, 128], BF)
        Vcar = sb("Vcar", [128, 4, 2, 128], BF)
        small = sb("small", [128, 16], F32)
        bnst = sb("bnst", [128, 4, 6], F32)
        ARENA = 96 * 1024
        arena = sb("arena", [128, ARENA // 2], BF)

        def carve(off, shape, dt):
            n = int(np.prod(shape[1:]))
            if dt == F32:
                assert off % 4 == 0
                v = arena[:, off // 2: off // 2 + 2 * n].bitcast(F32)
            else:
                v = arena[:, off // 2: off // 2 + n]
            if len(shape) == 3:
                v = v.rearrange("p (a b) -> p a b", a=shape[1])
            elif len(shape) == 4:
                v = v.rearrange("p (a b c) -> p a b c", a=shape[1], b=shape[2])
            return v
        K = 1024
        yT = carve(0, [128, 32, T], BF)
        oT = carve(0, [128, 16, T], BF)
        qT = carve(16 * K, [128, 4, T], BF)
        gT = carve(20 * K, [128, 4, T], BF)
        PT = [[carve(24 * K + (e * 2 + kb) * K, [128, 512], BF) for kb in range(2)] for e in range(2)]
        PT2 = [[carve(28 * K + (e * 2 + kb) * K, [128, 512], BF) for kb in range(2)] for e in range(2)]
        xres = carve(32 * K, [128, 4, D], F32)
        o = 32 * K
        xdt = carve(o, [128, 4, 512], BF); o += 4 * K
        xsD = carve(o, [128, 4, 512], BF); o += 4 * K
        siluz = carve(o, [128, 4, 512], BF); o += 4 * K
        MT = [carve(o + h * K, [128, 512], BF) for h in range(8)]; o += 8 * K
        LT = [carve(o + i * K, [128, 512], BF) for i in range(2)]; o += 2 * K
        CBT = carve(o, [128, 512], BF); o += K
        Btok = carve(o, [128, 4, 128], BF); o += K
        xdtw = carve(o, [128, 512], BF); o += K
        yn2 = [carve(o, [128, 512], BF), carve(o + K, [128, 512], BF)]; o += 2 * K
        ybuf = [carve(o, [128, 512], F32), carve(o + 2 * K, [128, 512], F32)]; o += 4 * K
        assert o <= 64 * K, o
        o = 64 * K
        BCT = carve(o, [128, 8, T], BF); o += 8 * K
        xsT2 = [carve(o, [128, 4, T], BF), carve(o + 4 * K, [128, 4, T], BF)]; o += 8 * K
        xpre2 = [carve(o, [128, 516], F32), carve(o + 2 * K + 16, [128, 516], F32)]; o += 4 * K + 32
        acc2 = [carve(o, [128, 512], F32), carve(o + 2 * K, [128, 512], F32)]; o += 4 * K
        dt_tok = carve(o, [128, 4, 64], F32); o += K
        negcum = carve(o, [128, 4, 64], F32); o += K
        expcum = carve(o, [128, 4, 64], F32); o += K
        dst = carve(o, [128, 4, 64], F32); o += K
        etot = carve(o, [128, 4, 64], F32); o += K
        cumT = carve(o, [128, 512], F32); o += 2 * K
        da = carve(o, [128, 64], F32); o += 256
        dtmp = carve(o, [128, 64], F32); o += 256
        assert o <= 96 * K, o
        o = 64 * K
        KT = carve(o, [128, 4, 640], BF); o += 5 * K
        Vp = carve(o, [128, 5, 4, 256], BF); o += 10 * K
        cosF = carve(o, [128, 512], F32); o += 2 * K
        sinF = carve(o, [128, 512], F32); o += 2 * K
        posi = carve(o, [128, 512], I32 if False else F32).bitcast(I32); o += 2 * K
        ang = carve(o, [128, 512], F32); o += 2 * K
        rtmp = carve(o, [128, 512], F32); o += 2 * K
        ktmp = posi
        qraw = carve(o, [128, 512], BF); o += K
        qa = carve(o, [128, 512], F32); o += 2 * K
        den = carve(o, [128, 512], F32); o += 2 * K
        attn = carve(o, [128, 512], F32); o += 2 * K
        assert o <= 96 * K, o
        xstage = sb("xstage", [128, 512], F32)

        ps = [es.enter_context(nc.psum_tensor(f"ps{i}", [128, 512], F32)) for i in range(8)]
        pb = [Buf(f"ps{i}", excl=True) for i in range(8)]

        b_cst, b_par, b_const2 = Buf(), Buf(), Buf()
        b_slab = [Buf(f"slab{i}") for i in range(NSLAB)]
        b_wbdt, b_wbv = Buf(), Buf()
        b_slot = [Buf(f"slot{i}") for i in range(NSLOT)]
        b_wdt, b_wv = Buf(), Buf()
        b_xT, b_xin, b_xstage = Buf("xT"), Buf("xin"), Buf("xstage")
        b_S = [Buf(f"S{g}") for g in range(8)]
        b_Sbf4 = [Buf() for _ in range(4)]
        b_halo = [Buf() for _ in range(48)]
        b_lng = [Buf(), Buf()]
        b_small = Buf()
        b_car = Buf()
        b_x1d = Buf()
        slot_rr = [0]

        S.dma("pool", cst[:], cst_d, writes=[b_cst])
        S.dma("pool", par[:], par_d, writes=[b_par])
        cb = [b_cst, b_par]
        S.op("dve", lambda e: e.tensor_copy(out=identb[:], in_=cst[:, C_ID:C_ID + 128]), reads=cb, writes=[b_const2])
        for j in range(4):
            S.op("dve", lambda e, j=j: e.tensor_copy(out=negm4[:, 128 * j:128 * j + 128], in_=cst[:, C_NEGM:C_NEGM + 128]), reads=cb, writes=[b_const2])
            S.op("dve", lambda e, j=j: e.tensor_copy(out=negmp4[:, 128 * j:128 * j + 128], in_=cst[:, C_NEGMP:C_NEGMP + 128]), reads=cb, writes=[b_const2])
        S.op("dve", lambda e: e.tensor_copy(out=pmatb[:], in_=cst[:, C_PMAT:C_PMAT + 128]), reads=cb, writes=[b_const2])
        S.op("dve", lambda e: e.tensor_copy(out=oneE[:], in_=cst[:, C_ONE_E:C_ONE_E + 128]), reads=cb, writes=[b_const2])
        S.op("dve", lambda e: e.tensor_copy(out=oneO[:], in_=cst[:, C_ONE_O:C_ONE_O + 128]), reads=cb, writes=[b_const2])
        S.op("act", lambda e: e.activation(out=Abc[:], in_=par[:, P_ALOG:P_ALOG + 64], func=AF.Exp), reads=cb, writes=[b_const2])
        S.op("act", lambda e: e.mul(out=Abc[:], in_=Abc[:], mul=-1.0), reads=cb, writes=[b_const2])
        S.op("act", lambda e: e.activation(out=expsink[:], in_=par[:, P_SINK:P_SINK + 16], func=AF.Exp), reads=cb, writes=[b_const2])
        S.op("pool", lambda e: e.memset(small[:, 8:9], EPS), writes=[b_small])
        S.op("pool", lambda e: e.memset(small[:, 9:10], -0.5), writes=[b_small])
        S.op("pool", lambda e: e.memset(halo[:], 0.0), writes=b_halo)
        S.op("pool", lambda e: e.memset(Sst[:], 0.0), writes=b_S)
        S.op("pool", lambda e: e.memset(Kcar[:], 0.0), writes=[b_car])
        S.op("pool", lambda e: e.memset(Vcar[:], 0.0), writes=[b_car])
        cb = [b_cst, b_par, b_const2]

        conv_src = {}
        conv_src[SL_B] = (win_d, 0, 8192); conv_src[SL_B + 1] = (win_d, 0, 8192 + 512)
        conv_src[SL_C] = (win_d, 0, 9216); conv_src[SL_C + 1] = (win_d, 0, 9216 + 512)
        for g in range(8):
            conv_src[SL_XS + g] = (win_d, 0, 4096 + 512 * g)
            conv_src[SL_Z + g] = (win_d, 0, 512 * g)
        for cs in range(4):
            for kh in range(2):
                conv_src[SL_WO + cs * 2 + kh] = (wout_d, 2048 * kh, 512 * cs)
        for g in range(4):
            conv_src[SL_Q + g] = (bwin_d, 0, 512 * g)
            conv_src[SL_G + g] = (bwin_d, 0, 2048 + 512 * g)
        for cs in range(4):
            conv_src[SL_O2 + cs] = (bwout_d, 0, 512 * cs)
        use_order = [SL_B, SL_C, SL_XS]
        for g in range(8):
            if g == 4:
                use_order += [SL_B + 1, SL_C + 1]
            if g + 1 < 8:
                use_order.append(SL_XS + g + 1)
            use_order.append(SL_Z + g)
        use_order += [SL_WO + i for i in range(8)] + [SL_K, "V"]
        for g in range(4):
            use_order += [SL_Q + g, SL_G + g]
        use_order += [SL_O2 + i for i in range(4)]
        conv_done = set()
        conv_ptr = [0]
        LOOKAHEAD = 10

        def do_convert(idx):
            if idx in conv_done:
                return
            conv_done.add(idx)
            if idx == "V":
                S.dma("pool", wbv_d, kvw_d[:, 256:512].rearrange("(kc p) c -> p kc c", p=128), writes=[b_wbv])
            elif idx == SL_K:
                for g in range(4):
                    ksrc = kvw_d[:, 64 * g:64 * g + 64].rearrange("(kc p) d -> p kc d", p=128)
                    for e2 in range(2):
                        S.dma("pool", wb_d[SL_K][:, :, 128 * g + 64 * e2:128 * g + 64 * e2 + 64], ksrc, writes=[b_slab[SL_K]])
            else:
                src, r0, c0 = conv_src[idx]
                v = src[r0:r0 + 2048, c0:c0 + 512].rearrange("(kc p) c -> p kc c", p=128)
                S.dma("pool", wb_d[idx], v, writes=[b_slab[idx]])

        def ensure_converted(idx):
            if idx not in conv_done:
                while conv_ptr[0] < len(use_order):
                    j = use_order[conv_ptr[0]]
                    conv_ptr[0] += 1
                    do_convert(j)
                    if j == idx:
                        break
                do_convert(idx)
            k = 0
            while conv_ptr[0] < len(use_order) and k < LOOKAHEAD:
                do_convert(use_order[conv_ptr[0]])
                conv_ptr[0] += 1
                k += 1

        S.dma("pool", wbdt_d, win_d[:, 10240:10304].rearrange("(kc p) c -> p kc c", p=128), writes=[b_wbdt])
        S.dma("sp", wdt[:], wbdt_d, reads=[b_wbdt], writes=[b_wdt])

        def load_slab(idx):
            ensure_converted(idx)
            i = slot_rr[0] % NSLOT
            slot_rr[0] += 1
            S.dma("sp", wslot[i][:], wb_d[idx], reads=[b_slab[idx]], writes=[b_slot[i]])
            return wslot[i], b_slot[i]

        evac_rr = [0]

        def mm_group(pbank, pbuf, mms, reads):
            n = len(mms)
            for i, (o_, l_, r_, st, sp_) in enumerate(mms):
                S.op("pe", lambda e, o_=o_, l_=l_, r_=r_, st=st, sp_=sp_: e.matmul(o_, lhsT=l_, rhs=r_, start=st, stop=sp_),
                     reads=reads, writes=[pbuf], signal=(i == n - 1))

        pj_rr = [0]

        pj_pending = {}

        def next_pj():
            i = pj_rr[0] % 2
            pj_rr[0] += 1
            if i in pj_pending:
                if (1 - i) not in pj_pending:
                    i = 1 - i
                    pj_rr[0] += 1
                else:
                    pj_pending.pop(i)()
            return ps[i], pb[i]

        try:
            for t in range(NT):
                tok0 = t * T
                bA = {n: Buf(n) for n in ["BCT", "xsT", "xdt", "xsD", "xstok", "siluz", "CBT", "Btok", "xdtw", "yn", "sqj",
                                          "toff", "yv", "y2", "xpre", "acc", "dtv", "cumT", "da", "dtmp", "yT", "LT0", "LT1"]}
                bMT = [Buf(f"MT{h}") for h in range(8)]
                bxsT = [Buf("xsT0"), Buf("xsT1")]
                bybuf = [Buf("yb0"), Buf("yb1")]
                byn2 = [Buf("yn0"), Buf("yn1")]
                bsmall2 = [Buf("sm0"), Buf("sm1")]
                bxp2 = [Buf("xp0"), Buf("xp1")]
                bac2 = [Buf("ac0"), Buf("ac1")]
                conv_rr = [0]
                S.dma("pool", lng[:, 0, :], lng_d[0], writes=[b_lng[0]])
                S.dma("pool", lng[:, 1, :], lng_d[1], writes=[b_lng[1]])
                pre_BC = [load_slab(SL_B), load_slab(SL_C)]
                for tb in range(4):
                    S.dma("sp", xin[:], x_d[tok0 + 128 * tb: tok0 + 128 * tb + 128, :], writes=[b_xin])
                    for q4 in range(4):
                        pbank, pbuf = ps[2 + (q4 % 2)], pb[2 + (q4 % 2)]
                        mm_group(pbank, pbuf, [(pbank[:, 128 * j:128 * j + 128], xin[:, (4 * q4 + j) * 128:(4 * q4 + j) * 128 + 128],
                                                cst[:, C_ID:C_ID + 128], True, True) for j in range(4)], reads=[b_xin] + cb)
                        eng = "act" if (q4 % 2) else "dve"
                        outap = xT[:, 4 * q4:4 * q4 + 4, 128 * tb:128 * tb + 128]
                        inap = pbank[:].rearrange("p (a b) -> p a b", a=4)
                        if eng == "act":
                            S.op("act", lambda e, o_=outap, i_=inap: e.activation(out=o_, in_=i_, func=AF.Copy), reads=[pbuf], writes=[b_xT])
                        else:
                            S.op("dve", lambda e, o_=outap, i_=inap: e.tensor_copy(out=o_, in_=i_), reads=[pbuf], writes=[b_xT])
                stage('A1')
                for tb in range(4):
                    pbank, pbuf = ps[3], pb[3]
                    mm_group(pbank, pbuf, [(pbank[:, 0:64], xT[:, kc, 128 * tb:128 * tb + 128], wdt[:, kc, :], kc == 0, kc == 15)
                                           for kc in range(16)], reads=[b_xT, b_wdt])
                    S.op("dve", lambda e: e.tensor_tensor(out=dtmp[:], in0=pbank[:, 0:64], in1=par[:, P_DTB:P_DTB + 64], op=ALU.add),
                         reads=[pbuf] + cb, writes=[bA["dtmp"]])
                    S.op("act", lambda e: e.activation(out=dtmp[:], in_=dtmp[:], func=AF.Exp), reads=[bA["dtmp"]], writes=[bA["dtmp"]])
                    S.op("act", lambda e, tb=tb: e.activation(out=dt_tok[:, tb, :], in_=dtmp[:], func=AF.Ln, bias=1.0),
                         reads=[bA["dtmp"]], writes=[bA["dtv"]])
                    S.op("dve", lambda e, tb=tb: e.tensor_tensor(out=da[:], in0=dt_tok[:, tb, :], in1=Abc[:], op=ALU.mult),
                         reads=[bA["dtv"]] + cb, writes=[bA["da"]])
                    mm_group(pbank, pbuf, [(pbank[:, 0:64], cst[:, C_U:C_U + 128], da[:], True, True),
                                           (pbank[:, 64:128], cst[:, C_ONES:C_ONES + 128], da[:], True, True),
                                           (pbank[0:64, 128:256], da[:], cst[:, C_U:C_U + 128], True, True)],
                             reads=[bA["da"]] + cb)
                    S.op("act", lambda e, tb=tb: e.mul(out=negcum[:, tb, :], in_=pbank[:, 0:64], mul=-1.0), reads=[pbuf], writes=[bA["dtv"]])
                    S.op("act", lambda e, tb=tb: e.activation(out=expcum[:, tb, :], in_=pbank[:, 0:64], func=AF.Exp), reads=[pbuf], writes=[bA["dtv"]])
                    S.op("act", lambda e, tb=tb: e.activation(out=etot[:, tb, :], in_=pbank[:, 64:128], func=AF.Exp), reads=[pbuf], writes=[bA["dtv"]])
                    S.op("dve", lambda e, tb=tb: e.tensor_tensor(out=dtmp[:], in0=pbank[:, 64:128], in1=negcum[:, tb, :], op=ALU.add),
                         reads=[pbuf, bA["dtv"]], writes=[bA["dtmp"]])
                    S.op("act", lambda e, tb=tb: e.activation(out=dst[:, tb, :], in_=dtmp[:], func=AF.Exp), reads=[bA["dtmp"]], writes=[bA["dtv"]])
                    S.op("dve", lambda e, tb=tb: e.tensor_copy(out=cumT[0:64, 128 * tb:128 * tb + 128], in_=pbank[0:64, 128:256]),
                         reads=[pbuf], writes=[bA["cumT"]])

                stage('A2')

                def proj_fm_conv(slot, bslot, fc, ch, dest, bdest, defer=False):
                    pbank, pbuf = next_pj()
                    ci = conv_rr[0] % 2
                    conv_rr[0] += 1
                    xpre, acc = xpre2[ci], acc2[ci]
                    bxpre, bacc = bxp2[ci], bac2[ci]
                    mm_group(pbank, pbuf, [(pbank[:], slot[:, kc, 128 * fc:128 * fc + 128], xT[:, kc, :], kc == 0, kc == 15)
                                           for kc in range(16)], reads=[bslot, b_xT])

                    bank_i = 0 if pbank is ps[0] else 1
                    done = [False]

                    def evac():
                        if done[0]:
                            return
                        done[0] = True
                        pj_pending.pop(bank_i, None)
                        proj_fm_conv_ev(pbank, pbuf, xpre, acc, bxpre, bacc, ch, dest, bdest)
                    if defer:
                        pj_pending[bank_i] = evac
                        return evac
                    evac()

                def proj_fm_conv_ev(pbank, pbuf, xpre, acc, bxpre, bacc, ch, dest, bdest):
                    S.op("pool", lambda e: e.tensor_copy(out=xpre[:, 0:3], in_=halo[:, ch, :]), reads=[b_halo[ch]], writes=[bxpre])
                    S.op("act", lambda e: e.activation(out=xpre[:, 3:515], in_=pbank[:], func=AF.Copy), reads=[pbuf], writes=[bxpre])
                    S.op("pool", lambda e: e.tensor_copy(out=halo[:, ch, :], in_=xpre[:, 512:515]), reads=[bxpre], writes=[b_halo[ch]])
                    S.op("act", lambda e: e.activation(out=acc[:], in_=pbank[:], func=AF.Identity,
                                                       bias=par[:, P_CB + ch:P_CB + ch + 1], scale=par[:, P_CW + 4 * ch + 3:P_CW + 4 * ch + 4]),
                         reads=[pbuf] + cb, writes=[bacc])
                    for j in (2, 1, 0):
                        S.op("dve", lambda e, j=j: e.scalar_tensor_tensor(out=acc[:], in0=xpre[:, j:j + 512],
                                                                          scalar=par[:, P_CW + 4 * ch + j:P_CW + 4 * ch + j + 1],
                                                                          in1=acc[:], op0=ALU.mult, op1=ALU.add),
                             reads=[bxpre, bacc] + cb, writes=[bacc])
                    S.op("act", lambda e: e.activation(out=dest, in_=acc[:], func=AF.Silu), reads=[bacc], writes=[bdest])

                xs_state = {}

                def xs_load(g):
                    xs_state[g] = load_slab(SL_XS + g)

                def xs_proj_chunk(g, fc, defer=False):
                    slot, bslot = xs_state[g]
                    return proj_fm_conv(slot, bslot, fc, 4 * g + fc, xsT2[g % 2][:, fc, :], bxsT[g % 2], defer=defer)

                for half in range(2):
                    slot, bslot = pre_BC[0] if half == 0 else load_slab(SL_B + half)
                    for fc in range(4):
                        proj_fm_conv(slot, bslot, fc, 32 + 4 * half + fc, BCT[:, fc, :], bA["BCT"])
                    slot, bslot = pre_BC[1] if half == 0 else load_slab(SL_C + half)
                    for fc in range(4):
                        proj_fm_conv(slot, bslot, fc, 40 + 4 * half + fc, BCT[:, 4 + fc, :], bA["BCT"])
                    if half == 0:
                        xs_load(0)
                        for fc in range(4):
                            xs_proj_chunk(0, fc)
                        if t > 0:
                            S.fence()
                    for gl in range(4):
                        g = 4 * half + gl
                        BTg = BCT[:, gl, :]
                        CTg = BCT[:, 4 + gl, :]
                        xsT = xsT2[g % 2]
                        bxs = bxsT[g % 2]
                        if g + 1 < 8:
                            xs_load(g + 1)
                        S.op("act", lambda e, g=g: e.activation(out=Sbf4[0][:], in_=Sst[:, g, :], func=AF.Copy), reads=[b_S[g]], writes=[b_Sbf4[0]])
                        for c in range(4):
                            pbank, pbuf = ps[2 + c % 2], pb[2 + c % 2]
                            mm_group(pbank, pbuf, [(pbank[:, 128 * fc:128 * fc + 128], xsT[:, fc, 128 * c:128 * c + 128], identb[:], True, True)
                                                   for fc in range(4)], reads=[bxs] + cb)
                            p3 = pbank[:].rearrange("p (h d) -> p h d", d=64)
                            S.op("dve", lambda e, c=c, g=g, p3=p3: e.tensor_tensor(out=xdt[:, c, :].rearrange("p (h d) -> p h d", d=64), in0=p3,
                                                                                   in1=bc(dt_tok[:, c, 8 * g:8 * g + 8], 64), op=ALU.mult),
                                 reads=[pbuf, bA["dtv"]], writes=[bA["xdt"]])
                            S.op("dve", lambda e, c=c, g=g, p3=p3: e.tensor_tensor(out=xsD[:, c, :].rearrange("p (h d) -> p h d", d=64), in0=p3,
                                                                                   in1=bc(par[:, P_AD + 8 * g:P_AD + 8 * g + 8], 64), op=ALU.mult),
                                 reads=[pbuf] + cb, writes=[bA["xsD"]])
                        pbank, pbuf = ps[2], pb[2]
                        mm_group(pbank, pbuf, [(pbank[:, 128 * c:128 * c + 128], BTg[:, 128 * c:128 * c + 128], identb[:], True, True)
                                               for c in range(4)], reads=[bA["BCT"]] + cb)
                        S.op("act", lambda e, pbank=pbank: e.activation(out=Btok[:].rearrange("p a b -> p (a b)"), in_=pbank[:], func=AF.Copy),
                             reads=[pbuf], writes=[bA["Btok"]])
                        pbank, pbuf = ps[3], pb[3]
                        mm_group(pbank, pbuf, [(pbank[:, 128 * c:128 * c + 128], BTg[:, 128 * c:128 * c + 128], CTg[:, 128 * c:128 * c + 128], True, True)
                                               for c in range(4)], reads=[bA["BCT"]])
                        S.op("act", lambda e, pbank=pbank: e.activation(out=CBT[:], in_=pbank[:], func=AF.Copy), reads=[pbuf], writes=[bA["CBT"]])
                        zslot = load_slab(SL_Z + g)

                        def FILLZ(tb, zslot=zslot):
                            slot, bslot = zslot
                            pbank, pbuf = next_pj()
                            mm_group(pbank, pbuf, [(pbank[:], xT[:, kc, 128 * tb:128 * tb + 128], slot[:, kc, :], kc == 0, kc == 15)
                                                   for kc in range(16)], reads=[bslot, b_xT])
                            S.op("act", lambda e: e.activation(out=siluz[:, tb, :], in_=pbank[:], func=AF.Silu),
                                 reads=[pbuf], writes=[bA["siluz"]])

                        def state_step(c, g=g):
                            S.op("dve", lambda e: e.tensor_tensor(out=xdtw[:].rearrange("p (h d) -> p h d", d=64),
                                                                            in0=xdt[:, c, :].rearrange("p (h d) -> p h d", d=64),
                                                                            in1=bc(dst[:, c, 8 * g:8 * g + 8], 64), op=ALU.mult),
                                 reads=[bA["xdt"], bA["dtv"]], writes=[bA["xdtw"]])
                            pS, bpS = ps[6 + c % 2], pb[6 + c % 2]
                            mm_group(pS, bpS, [(pS[:], Btok[:, c, :], xdtw[:], True, True)], reads=[bA["Btok"], bA["xdtw"]])
                            S.op("dve", lambda e: e.tensor_tensor(out=Sst[:, g, :].rearrange("p (h d) -> p h d", d=64),
                                                                            in0=Sst[:, g, :].rearrange("p (h d) -> p h d", d=64),
                                                                            in1=bc(etot[:, c, 8 * g:8 * g + 8], 64), op=ALU.mult),
                                 reads=[b_S[g], bA["dtv"]], writes=[b_S[g]])
                            S.op("dve", lambda e: e.tensor_tensor(out=Sst[:, g, :], in0=pS[:], in1=Sst[:, g, :], op=ALU.add),
                                 reads=[bpS, b_S[g]], writes=[b_S[g]])
                            if c < 3:
                                S.op("act", lambda e: e.activation(out=Sbf4[c + 1][:], in_=Sst[:, g, :], func=AF.Copy),
                                     reads=[b_S[g]], writes=[b_Sbf4[c + 1]])
                        state_step(0); FILLZ(0); state_step(1); state_step(2); FILLZ(1); state_step(3)
                        for h in range(8):
                            hh = 8 * g + h
                            pbank, pbuf = ps[4 + (h % 2)], pb[4 + (h % 2)]
                            sel = cst[0:64, C_ID + hh:C_ID + hh + 1].broadcast_to([64, 128])
                            mm_group(pbank, pbuf, [(pbank[:], sel, cumT[0:64, :], True, False),
                                                   (pbank[:], identb[:], negm4[:], False, True)], reads=[bA["cumT"]] + cb)
                            lt, blt = LT[h % 2], bA[f"LT{h % 2}"]
                            for c in range(4):
                                S.op("act", lambda e, c=c, hh=hh, lt=lt, pbank=pbank: e.activation(
                                    out=lt[:, 128 * c:128 * c + 128], in_=pbank[:, 128 * c:128 * c + 128], func=AF.Exp,
                                    bias=negcum[:, c, hh:hh + 1], scale=1.0), reads=[pbuf, bA["dtv"]], writes=[blt])
                            S.op("dve", lambda e, h=h, lt=lt: e.tensor_tensor(out=MT[h][:], in0=lt[:], in1=CBT[:], op=ALU.mult),
                                 reads=[blt, bA["CBT"]], writes=[bMT[h]])
                        def Y1(c, g=g, CTg=CTg):
                            cs_ = slice(128 * c, 128 * c + 128)
                            yb, byb = ybuf[c % 2], bybuf[c % 2]
                            pA, bpA = (ps[6], pb[6]) if c % 2 == 0 else (ps[3], pb[3])
                            pB, bpB = (ps[7], pb[7]) if c % 2 == 0 else (ps[4], pb[4])
                            mms = [(pA[:], identb[:], xsD[:, c, :], True, False)]
                            for h in range(8):
                                mms.append((pA[:, 64 * h:64 * h + 64], MT[h][:, cs_], xdt[:, c, 64 * h:64 * h + 64], False, h == 7))
                            mm_group(pA, bpA, mms, reads=[bA["xsD"], bA["xdt"]] + bMT + cb)
                            mm_group(pB, bpB, [(pB[:], CTg[:, cs_], Sbf4[c][:], True, True)], reads=[bA["BCT"], b_Sbf4[c]])
                            S.op("dve", lambda e: e.tensor_tensor(out=yb[:].rearrange("p (h d) -> p h d", d=64),
                                                                  in0=pB[:].rearrange("p (h d) -> p h d", d=64),
                                                                  in1=bc(expcum[:, c, 8 * g:8 * g + 8], 64), op=ALU.mult),
                                 reads=[bpB, bA["dtv"]], writes=[byb])
                            S.op("dve", lambda e: e.tensor_tensor(out=yb[:], in0=pA[:], in1=yb[:], op=ALU.add),
                                 reads=[bpA, byb], writes=[byb])
                            S.op("dve", lambda e: e.tensor_tensor(out=yb[:], in0=yb[:], in1=siluz[:, c, :], op=ALU.mult),
                                 reads=[byb, bA["siluz"]], writes=[byb])

                        def S2(c):
                            yb, byb = ybuf[c % 2], bybuf[c % 2]
                            ynb, bynb = yn2[c % 2], byn2[c % 2]
                            sm, bsm = small[:, 10 + 3 * (c % 2):13 + 3 * (c % 2)], bsmall2[c % 2]
                            S.op("dve", lambda e: e.scalar_tensor_tensor(out=ynb[:], in0=yb[:], scalar=1.0, in1=yb[:], op0=ALU.mult, op1=ALU.mult,
                                                                         accum_out=sm[:, 0:1]), reads=[byb], writes=[bynb, bsm])
                            S.op("dve", lambda e: e.tensor_scalar(out=sm[:, 1:2], in0=sm[:, 0:1], scalar1=1.0 / 512.0, scalar2=EPS, op0=ALU.mult, op1=ALU.add),
                                 reads=[bsm], writes=[bsm])
                            S.op("pool", lambda e: e.tensor_tensor(out=sm[:, 2:3], in0=sm[:, 1:2], in1=small[:, 9:10], op=ALU.pow),
                                 reads=[bsm, b_small], writes=[bsm])
                            S.op("act", lambda e: e.activation(out=ynb[:], in_=yb[:], func=AF.Identity, scale=sm[:, 2:3]),
                                 reads=[byb, bsm], writes=[bynb])

                        def S3(c, g=g):
                            cs_ = slice(128 * c, 128 * c + 128)
                            ynb, bynb = yn2[c % 2], byn2[c % 2]
                            pT2, bpT2 = (ps[2], pb[2]) if c % 2 == 0 else (ps[5], pb[5])
                            mm_group(pT2, bpT2, [(pT2[:, 128 * fc:128 * fc + 128], ynb[:, 128 * fc:128 * fc + 128], identb[:], True, True)
                                                 for fc in range(4)], reads=[bynb] + cb)
                            for fc in range(4):
                                nwc = par[:, P_NW + 4 * g + fc:P_NW + 4 * g + fc + 1]
                                if fc < 2:
                                    S.op("act", lambda e, fc=fc: e.activation(out=yT[:, 4 * g + fc, cs_], in_=pT2[:, 128 * fc:128 * fc + 128],
                                                                              func=AF.Identity, scale=nwc), reads=[bpT2] + cb, writes=[bA["yT"]])
                                else:
                                    S.op("dve", lambda e, fc=fc: e.tensor_scalar(out=yT[:, 4 * g + fc, cs_], in0=pT2[:, 128 * fc:128 * fc + 128],
                                                                                 scalar1=nwc, scalar2=None, op0=ALU.mult), reads=[bpT2] + cb, writes=[bA["yT"]])

                        fill_ev = {}

                        def FILL(c, g=g):
                            if g + 1 < 8:
                                fill_ev[c] = xs_proj_chunk(g + 1, c, defer=True)

                        def FILLEV(c):
                            if c in fill_ev:
                                fill_ev.pop(c)()

                        Y1(0); S2(0)
                        Y1(1); S2(1); FILLZ(2); FILL(0); S3(0); FILLEV(0)
                        Y1(2); S2(2); FILLZ(3); FILL(1); S3(1); FILLEV(1)
                        Y1(3); S2(3); FILL(2); S3(2); FILLEV(2)
                        FILL(3); S3(3); FILLEV(3)

                stage('A')
                S.fence()
                bC = {n: Buf(n) for n in ["KT", "Vp", "rope", "qraw", "qa", "qT", "gT", "oT", "den", "attn", "tmp"]}
                bPT = [[Buf(), Buf()], [Buf(), Buf()]]
                bPT2 = [[Buf(), Buf()], [Buf(), Buf()]]
                pos_bc = AP(pos_d.tensor, tok0, [[0, 128], [1, 512]])
                S.dma("pool", posi, pos_bc, writes=[bC["rope"]])
                S.op("dve", lambda e: e.tensor_copy(out=ang[:], in_=posi), reads=[bC["rope"]], writes=[bC["rope"]])
                S.op("dve", lambda e: e.tensor_scalar(out=ang[:], in0=ang[:], scalar1=cst[:, C_FREQ:C_FREQ + 1], scalar2=None, op0=ALU.mult),
                     reads=[bC["rope"]] + cb, writes=[bC["rope"]])

                def sincos(dest, shift, scale_ap):
                    S.op("dve", lambda e: e.tensor_scalar(out=rtmp[:], in0=ang[:], scalar1=shift, scalar2=1.0 / (2 * math.pi),
                                                          op0=ALU.add, op1=ALU.mult), reads=[bC["rope"]], writes=[bC["tmp"]])
                    S.op("dve", lambda e: e.tensor_scalar(out=rtmp[:], in0=rtmp[:], scalar1=0.5, scalar2=None, op0=ALU.add),
                         reads=[bC["tmp"]], writes=[bC["tmp"]])
                    S.op("dve", lambda e: e.tensor_copy(out=ktmp, in_=rtmp[:]), reads=[bC["tmp"]], writes=[bC["tmp"]])
                    S.op("dve", lambda e: e.tensor_copy(out=rtmp[:], in_=ktmp), reads=[bC["tmp"]], writes=[bC["tmp"]])
                    S.op("dve", lambda e: e.scalar_tensor_tensor(out=rtmp[:], in0=rtmp[:], scalar=-2 * math.pi, in1=ang[:], op0=ALU.mult, op1=ALU.add),
                         reads=[bC["tmp"], bC["rope"]], writes=[bC["tmp"]])
                    S.op("dve", lambda e: e.tensor_scalar(out=rtmp[:], in0=rtmp[:], scalar1=shift, scalar2=None, op0=ALU.add),
                         reads=[bC["tmp"]], writes=[bC["tmp"]])
                    S.op("dve", lambda e: e.tensor_scalar(out=qa[:], in0=rtmp[:], scalar1=-math.pi, scalar2=2 * math.pi, op0=ALU.is_lt, op1=ALU.mult),
                         reads=[bC["tmp"]], writes=[bC["qa"]])
                    S.op("dve", lambda e: e.tensor_tensor(out=rtmp[:], in0=rtmp[:], in1=qa[:], op=ALU.add), reads=[bC["tmp"], bC["qa"]], writes=[bC["tmp"]])
                    S.op("dve", lambda e: e.tensor_scalar(out=qa[:], in0=rtmp[:], scalar1=math.pi, scalar2=-2 * math.pi, op0=ALU.is_gt, op1=ALU.mult),
                         reads=[bC["tmp"]], writes=[bC["qa"]])
                    S.op("dve", lambda e: e.tensor_tensor(out=rtmp[:], in0=rtmp[:], in1=qa[:], op=ALU.add), reads=[bC["tmp"], bC["qa"]], writes=[bC["tmp"]])
                    S.op("dve", lambda e: e.tensor_scalar(out=rtmp[:], in0=rtmp[:], scalar1=-3.1415925, scalar2=3.1415925, op0=ALU.max, op1=ALU.min),
                         reads=[bC["tmp"]], writes=[bC["tmp"]])
                    if scale_ap is None:
                        S.op("act", lambda e: e.activation(out=dest[:], in_=rtmp[:], func=AF.Sin), reads=[bC["tmp"]], writes=[bC["rope"]])
                    else:
                        S.op("act", lambda e: e.activation(out=dest[:], in_=rtmp[:], func=AF.Sin, scale=scale_ap), reads=[bC["tmp"]] + cb, writes=[bC["rope"]])
                sincos(cosF, math.pi / 2, None)
                sincos(sinF, 0.0, cst[:, C_SGN:C_SGN + 1])

                S.op("pool", lambda e: e.memset(Vp[:], 0.0), writes=[bC["Vp"]])
                S.op("pool", lambda e: e.tensor_copy(out=KT[:, :, 0:128], in_=Kcar[:]), reads=[b_car], writes=[bC["KT"]])
                S.op("pool", lambda e: e.tensor_copy(out=Vp[:, 0, :, :].rearrange("p g (v d) -> p g v d", v=2), in_=Vcar[:]), reads=[b_car], writes=[bC["Vp"]])

                b_xres = [Buf(f"xres{i}") for i in range(4)]
                for cs in range(4):
                    for kh in range(2):
                        slot, bslot = load_slab(SL_WO + cs * 2 + kh)
                        for tb in range(4):
                            pO, bpO = ps[4 + tb], pb[4 + tb]
                            mm_group(pO, bpO, [(pO[:], yT[:, 16 * kh + kc, 128 * tb:128 * tb + 128], slot[:, kc, :],
                                                (kh == 0 and kc == 0), (kh == 1 and kc == 15)) for kc in range(16)],
                                     reads=[bslot, bA["yT"]])
                    for tb in range(4):
                        pO, bpO = ps[4 + tb], pb[4 + tb]
                        r0 = tok0 + 128 * tb
                        S.dma("pool", xstage[:], x_d[r0:r0 + 128, 512 * cs:512 * cs + 512], writes=[b_xstage])
                        S.op("dve", lambda e, tb=tb, cs=cs, pO=pO: e.scalar_tensor_tensor(
                            out=xres[:, tb, 512 * cs:512 * cs + 512], in0=xstage[:], scalar=ALPHA, in1=pO[:], op0=ALU.mult, op1=ALU.add),
                            reads=[b_xstage, bpO], writes=[b_xres[tb]])

                def layer_norm(tb, li):
                    v = xres[:, tb, :]
                    for j in range(4):
                        S.op("dve", lambda e, j=j, v=v: e.bn_stats(out=bnst[:, j, :], in_=v[:, 512 * j:512 * j + 512]),
                             reads=[b_xres[tb]], writes=[b_small])
                    S.op("dve", lambda e: e.bn_aggr(out=small[:, 4:6], in_=bnst[:].rearrange("p a b -> p (a b)")), reads=[b_small], writes=[b_small])
                    S.op("dve", lambda e: e.tensor_scalar(out=small[:, 6:7], in0=small[:, 5:6], scalar1=EPS, scalar2=None, op0=ALU.add),
                         reads=[b_small], writes=[b_small])
                    S.op("pool", lambda e: e.tensor_tensor(out=small[:, 7:8], in0=small[:, 6:7], in1=small[:, 9:10], op=ALU.pow),
                         reads=[b_small], writes=[b_small])
                    S.op("dve", lambda e, v=v: e.scalar_tensor_tensor(out=v, in0=v, scalar=small[:, 4:5], in1=lng[:, 0, :],
                                                                      op0=ALU.subtract, op1=ALU.mult),
                         reads=[b_small, b_xres[tb], b_lng[0]], writes=[b_xres[tb]])
                    S.op("dve", lambda e, v=v: e.scalar_tensor_tensor(out=v, in0=v, scalar=small[:, 7:8], in1=lng[:, 1, :],
                                                                      op0=ALU.mult, op1=ALU.add),
                         reads=[b_small, b_xres[tb], b_lng[1]], writes=[b_xres[tb]])

                for tb in range(4):
                    layer_norm(tb, 0)
                    if dbg:
                        S.dma("pool", dbg_d[tok0 + 128 * tb:tok0 + 128 * tb + 128, :], xres[:, tb, :], reads=[b_xres[tb]], is_output=True)
                    for q4 in range(4):
                        pbank, pbuf = ps[2 + (q4 % 2)], pb[2 + (q4 % 2)]
                        mm_group(pbank, pbuf, [(pbank[:, 128 * j:128 * j + 128], xres[:, tb, (4 * q4 + j) * 128:(4 * q4 + j) * 128 + 128],
                                                cst[:, C_ID:C_ID + 128], True, True) for j in range(4)], reads=[b_xres[tb]] + cb)
                        outap = xT[:, 4 * q4:4 * q4 + 4, 128 * tb:128 * tb + 128]
                        inap = pbank[:].rearrange("p (a b) -> p a b", a=4)
                        if q4 % 2:
                            S.op("act", lambda e, o_=outap, i_=inap: e.activation(out=o_, in_=i_, func=AF.Copy), reads=[pbuf], writes=[b_xT])
                        else:
                            S.op("dve", lambda e, o_=outap, i_=inap: e.tensor_copy(out=o_, in_=i_), reads=[pbuf], writes=[b_xT])
                S.dma("pool", lng[:, 0, :], lng_d[2], writes=[b_lng[0]])
                S.dma("pool", lng[:, 1, :], lng_d[3], writes=[b_lng[1]])

                stage('B')
                S.fence()
                def rope_evac(pbank, pbuf, bias_ap, dest, bdest):
                    S.op("act", lambda e: e.activation(out=qraw[:], in_=pbank[:], func=AF.Identity, bias=bias_ap), reads=[pbuf] + cb, writes=[bC["qraw"]])
                    pP, bpP = ps[2], pb[2]
                    mm_group(pP, bpP, [(pP[:], pmatb[:], qraw[:], True, True)], reads=[bC["qraw"]] + cb)
                    S.op("pool", lambda e: e.tensor_tensor(out=qa[:], in0=qraw[:], in1=cosF[:], op=ALU.mult), reads=[bC["qraw"], bC["rope"]], writes=[bC["qa"]])
                    S.op("dve", lambda e: e.tensor_tensor(out=rtmp[:], in0=pP[:], in1=sinF[:], op=ALU.mult), reads=[bpP, bC["rope"]], writes=[bC["tmp"]])
                    S.op("dve", lambda e: e.tensor_tensor(out=dest, in0=rtmp[:], in1=qa[:], op=ALU.add), reads=[bC["tmp"], bC["qa"]], writes=[bdest])

                slot, bslot = load_slab(SL_K)
                for g in range(4):
                    pbank, pbuf = next_pj()
                    mm_group(pbank, pbuf, [(pbank[:], slot[:, kc, 128 * g:128 * g + 128], xT[:, kc, :], kc == 0, kc == 15) for kc in range(16)],
                             reads=[bslot, b_xT])
                    rope_evac(pbank, pbuf, par[:, P_KB + g:P_KB + g + 1], KT[:, g, 128:640], bC["KT"])
                ensure_converted("V")
                i_ = slot_rr[0] % NSLOT
                slot_rr[0] += 1
                S.dma("sp", wslot[i_][:, :, 0:256], wbv_d, reads=[b_wbv], writes=[b_slot[i_]])
                wv, b_wv = wslot[i_], b_slot[i_]
                for tb in range(4):
                    pbank, pbuf = next_pj()
                    mm_group(pbank, pbuf, [(pbank[:, 0:256], xT[:, kc, 128 * tb:128 * tb + 128], wv[:, kc, 0:256], kc == 0, kc == 15) for kc in range(16)],
                             reads=[b_wv, b_xT])
                    vv = Vp[:, 1 + tb, :, :].rearrange("p g (v d) -> p g v d", v=2)
                    pv = pbank[:, 0:256].rearrange("p (g d) -> p g d", d=64)
                    vb = par[:, P_VB:P_VB + 256].rearrange("p (g d) -> p g d", d=64)
                    S.op("dve", lambda e, vv=vv, pv=pv, vb=vb: e.tensor_tensor(out=vv[:, :, 0, 0:64], in0=pv, in1=vb, op=ALU.add),
                         reads=[pbuf] + cb, writes=[bC["Vp"]])
                    S.op("dve", lambda e, vv=vv, pv=pv, vb=vb: e.tensor_tensor(out=vv[:, :, 1, 64:128], in0=pv, in1=vb, op=ALU.add),
                         reads=[pbuf] + cb, writes=[bC["Vp"]])
                S.op("pool", lambda e: e.tensor_copy(out=Kcar[:], in_=KT[:, :, 512:640]), reads=[bC["KT"]], writes=[b_car])
                S.op("pool", lambda e: e.tensor_copy(out=Vcar[:], in_=Vp[:, 4, :, :].rearrange("p g (v d) -> p g v d", v=2)), reads=[bC["Vp"]], writes=[b_car])

                stage('C1')
                blk_i = 0
                for g in range(4):
                    slot, bslot = load_slab(SL_Q + g)
                    for fc in range(4):
                        pbank, pbuf = next_pj()
                        mm_group(pbank, pbuf, [(pbank[:], slot[:, kc, 128 * fc:128 * fc + 128], xT[:, kc, :], kc == 0, kc == 15) for kc in range(16)],
                                 reads=[bslot, b_xT])
                        rope_evac(pbank, pbuf, par[:, P_QB + 4 * g + fc:P_QB + 4 * g + fc + 1], qT[:, fc, :], bC["qT"])
                    slot, bslot = load_slab(SL_G + g)
                    for fc in range(4):
                        pbank, pbuf = next_pj()
                        mm_group(pbank, pbuf, [(pbank[:], slot[:, kc, 128 * fc:128 * fc + 128], xT[:, kc, :], kc == 0, kc == 15) for kc in range(16)],
                                 reads=[bslot, b_xT])
                        S.op("act", lambda e, fc=fc, pbank=pbank: e.activation(out=gT[:, fc, :], in_=pbank[:], func=AF.Silu), reads=[pbuf], writes=[bC["gT"]])
                    for c in range(4):
                        kbs = [0, 1] if not (t == 0 and c == 0) else [1]
                        PTs, bPTs = (PT, bPT) if (blk_i % 2 == 0) else (PT2, bPT2)
                        blk_i += 1
                        for e2 in range(2):
                            for kb in kbs:
                                pS_, bpS_ = ps[3 + (e2 * 2 + kb) % 2], pb[3 + (e2 * 2 + kb) % 2]
                                kcol = 128 * c + 128 * kb
                                mm_group(pS_, bpS_, [(pS_[:].rearrange("p (a b) -> p a b", a=4), KT[64 * e2:64 * e2 + 64, g, kcol:kcol + 128],
                                                      qT[64 * e2:64 * e2 + 64, :, 128 * c:128 * c + 128], True, False),
                                                     (pS_[:], identb[:], (negm4 if kb == 1 else negmp4)[:], False, True)],
                                         reads=[bC["KT"], bC["qT"]] + cb)
                                S.op("act", lambda e, pS_=pS_, dest=PTs[e2][kb]: e.activation(out=dest[:], in_=pS_[:], func=AF.Exp, scale=0.125),
                                     reads=[bpS_], writes=[bPTs[e2][kb]])
                        pO, bpO = ps[5], pb[5]
                        pSm, bpSm = ps[6], pb[6]
                        combos = [(e2, kb) for e2 in range(2) for kb in kbs]
                        mmo, mms_ = [], []
                        for i, (e2, kb) in enumerate(combos):
                            vblk = c + kb
                            mmo.append((pO[:], Vp[:, vblk, g, 128 * e2:128 * e2 + 128], PTs[e2][kb][:], i == 0, i == len(combos) - 1))
                            mms_.append((pSm[:], (oneE if e2 == 0 else oneO)[:], PTs[e2][kb][:], i == 0, i == len(combos) - 1))
                        rds = [bC["Vp"]] + [bPTs[e2][kb] for (e2, kb) in combos] + cb
                        mm_group(pO, bpO, mmo, reads=rds)
                        mm_group(pSm, bpSm, mms_, reads=rds)
                        S.op("dve", lambda e, g=g, pSm=pSm: e.tensor_tensor(out=den[:].rearrange("p (a b) -> p a b", a=4),
                                                                            in0=pSm[:].rearrange("p (a b) -> p a b", a=4),
                                                                            in1=bc(expsink[:, 4 * g:4 * g + 4], 128), op=ALU.add),
                             reads=[bpSm] + cb, writes=[bC["den"]])
                        S.op("dve", lambda e: e.reciprocal(out=den[:], in_=den[:]), reads=[bC["den"]], writes=[bC["den"]])
                        S.op("dve", lambda e, pO=pO: e.tensor_tensor(out=attn[:], in0=pO[:], in1=den[:], op=ALU.mult), reads=[bpO, bC["den"]], writes=[bC["attn"]])
                        S.op("pool", lambda e, g=g, c=c: e.tensor_tensor(out=oT[:, 4 * g:4 * g + 4, 128 * c:128 * c + 128],
                                                                         in0=attn[:].rearrange("p (a b) -> p a b", a=4),
                                                                         in1=gT[:, :, 128 * c:128 * c + 128], op=ALU.mult),
                             reads=[bC["attn"], bC["gT"]], writes=[bC["oT"]])
                stage('C2')
                for cs in range(4):
                    slot, bslot = load_slab(SL_O2 + cs)
                    for tb in range(4):
                        pO, bpO = ps[4 + tb], pb[4 + tb]
                        mm_group(pO, bpO, [(pO[:], oT[:, kc, 128 * tb:128 * tb + 128], slot[:, kc, :], kc == 0, kc == 15) for kc in range(16)],
                                 reads=[bslot, bC["oT"]])
                        S.op("dve", lambda e, tb=tb, cs=cs, pO=pO: e.scalar_tensor_tensor(
                            out=xres[:, tb, 512 * cs:512 * cs + 512], in0=xres[:, tb, 512 * cs:512 * cs + 512], scalar=ALPHA, in1=pO[:],
                            op0=ALU.mult, op1=ALU.add), reads=[bpO, b_xres[tb]], writes=[b_xres[tb]])
                for tb in range(4):
                    layer_norm(tb, 1)
                    r0 = tok0 + 128 * tb
                    S.dma("pool", out_d[r0:r0 + 128, :], xres[:, tb, :], reads=[b_xres[tb]], is_output=True)

        except _Stop:
            pass
        with nc.Block() as block:
            S.emit(block)
    return nc


def _consts():
    c = np.zeros((128, NCST), np.float32)
    c[:, C_ID:C_ID + 128] = np.eye(128)
    i = np.arange(128)
    c[:, C_U:C_U + 128] = (i[:, None] <= i[None, :])
    c[:, C_ONES:C_ONES + 128] = 1.0
    c[:, C_NEGM:C_NEGM + 128] = np.where(i[:, None] <= i[None, :], 0.0, NEG)
    c[:, C_NEGMP:C_NEGMP + 128] = np.where(i[:, None] > i[None, :], 0.0, NEG)
    pm = np.zeros((128, 128), np.float32)
    for f2 in range(128):
        d = f2 % 64
        if d < 8:
            pm[f2 + 8, f2] = 1.0
        elif d < 16:
            pm[f2 - 8, f2] = 1.0
    c[:, C_PMAT:C_PMAT + 128] = pm
    c[:, C_ONE_E:C_ONE_E + 64] = 1.0
    c[:, C_ONE_O + 64:C_ONE_O + 128] = 1.0
    inv = (500000.0 ** (-np.arange(0, 16, 2, dtype=np.float32) / 16)).astype(np.float32)
    for p in range(128):
        d = p % 64
        if d < 16:
            c[p, C_FREQ] = inv[d % 8]
            c[p, C_SGN] = -1.0 if d < 8 else 1.0
    return c


def _params(a_conv_w, a_conv_b, a_dt_bias, a_log, a_d, a_norm_w, kv_b, b_q_bias, b_sinks):
    p = np.zeros((128, NPAR), np.float32)
    cw = a_conv_w[0].reshape(4, 48, 128)
    p[:, P_CW:P_CW + 192] = cw.transpose(2, 1, 0).reshape(128, 192)
    p[:, P_CB:P_CB + 48] = a_conv_b[0].reshape(48, 128).T
    p[:, P_DTB:P_DTB + 64] = a_dt_bias[0][None, :]
    p[:, P_ALOG:P_ALOG + 64] = a_log[0][None, :]
    p[:, P_AD:P_AD + 64] = a_d[0][None, :]
    p[:, P_NW:P_NW + 32] = a_norm_w[0].reshape(32, 128).T
    kb = kv_b[:256].reshape(4, 64)
    p[:, P_KB:P_KB + 4] = np.concatenate([kb, kb], axis=1).T
    p[:, P_VB:P_VB + 256] = kv_b[256:][None, :]
    p[:, P_QB:P_QB + 16] = b_q_bias[0].reshape(16, 128).T
    sk = b_sinks[0].reshape(16, 2)
    p[:64, P_SINK:P_SINK + 16] = sk[:, 0][None, :]
    p[64:, P_SINK:P_SINK + 16] = sk[:, 1][None, :]
    return p


def make_in_maps(inputs, cores):
    f = lambda a: np.ascontiguousarray(np.asarray(a, dtype=np.float32))
    cst = _consts()
    par = _params(*[np.asarray(inputs[k], np.float32) for k in
                    ["a_conv_w", "a_conv_b", "a_dt_bias", "a_log", "a_d", "a_norm_w", "kv_b", "b_q_bias", "b_sinks"]])
    g = np.asarray(inputs["ln_g"], np.float32)
    b = np.asarray(inputs["ln_b"], np.float32)
    lng = np.stack([np.broadcast_to(v[None, :], (128, D)) for v in (g[0], b[0], g[1], b[1])]).astype(np.float32)
    shared = {"w_in": f(inputs["a_w_in"][0]), "w_out": f(inputs["a_w_out"][0]), "kv_w": f(inputs["kv_w"]),
              "bw_in": f(inputs["b_w_in"][0]), "bw_out": f(inputs["b_w_out"][0]), "par": par, "cst": cst,
              "lng": np.ascontiguousarray(lng)}
    maps = []
    for c in cores:
        m = dict(shared)
        m["x"] = f(inputs["x"][c])
        m["pos"] = np.ascontiguousarray(np.asarray(inputs["positions"][c], np.int32).reshape(1, SEQ))
        maps.append(m)
    return maps


def kernel(**inputs):
    nc = build(NT=SEQ // T)
    maps = make_in_maps(inputs, list(range(8)))
    res = run_bass_kernel_spmd(nc, maps, core_ids=list(range(8)))
    return np.stack([r["out"] for r in res.results], axis=0).astype(np.float32)
```

```python
import math
from contextlib import ExitStack
import numpy as np
import concourse.bass as bass
import concourse.mybir as mybir
from concourse.ap import AP
from concourse.bass_utils import run_bass_kernel_spmd

F32 = mybir.dt.float32
BF = mybir.dt.bfloat16
I32 = mybir.dt.int32
AF = mybir.ActivationFunctionType
ALU = mybir.AluOpType

D = 2048
SEQ = 4096
T = 512
DIN = 4096
NPROJ = 10304
ALPHA = (2.0 * 2) ** 0.25
EPS = 1e-5
NEG = -30000.0

C_ID, C_U, C_ONES, C_NEGM, C_NEGMP, C_PMAT, C_ONE_E, C_ONE_O, C_FREQ, C_SGN = 0, 128, 256, 384, 512, 640, 768, 896, 1024, 1025
NCST = 1028
P_CW, P_CB, P_DTB, P_ALOG, P_AD, P_NW, P_KB, P_VB, P_QB, P_SINK = 0, 192, 240, 304, 368, 432, 464, 468, 724, 740
NPAR = 756

SL_Z, SL_XS, SL_B, SL_C, SL_WO, SL_K, SL_Q, SL_G, SL_O2 = 0, 8, 16, 18, 20, 28, 29, 33, 37
NSLAB = 41


class Buf:
    __slots__ = ("name", "w", "r", "excl")

    def __init__(self, name="", excl=False):
        self.name = name
        self.w = None
        self.r = []
        self.excl = excl


class _Rec:
    def __getattr__(self, name):
        def f(*a, **k):
            self.call = (name, a, k)
            return self
        return f


class Sched:
    CH = 2000
    ENGS = ("pe", "act", "dve", "pool", "sp")
    SAME_SYNC = {"pe": False, "act": True, "dve": True, "pool": True, "sp": False}

    def __init__(self, nc, es, ring=16):
        self.nc = nc
        self.es = es
        self.q = {e: [] for e in self.ENGS}
        self.cnt = {e: 0 for e in self.ENGS}
        self.sem = {e: None for e in self.ENGS}
        self.waited = {e: {} for e in self.ENGS}
        self.nsem = 0
        self.ring = {e: [] for e in ("sp", "pool", "act")}
        self.ringn = {e: 0 for e in ("sp", "pool", "act")}
        self.ringsz = ring
        self.out_tokens = []

    def _newsem(self, name):
        self.nsem += 1
        return self.es.enter_context(self.nc.semaphore(f"{name}{self.nsem}"))

    def _need(self, eng, tok, waits):
        if tok is None:
            return
        src, sem, val = tok
        if src == eng and not self.SAME_SYNC[eng]:
            return
        key = id(sem)
        if self.waited[eng].get(key, 0) >= val:
            return
        self.waited[eng][key] = val
        waits.append((sem, val))

    def _deps(self, eng, reads, writes):
        waits = []
        for b in reads:
            self._need(eng, b.w, waits)
            if b.excl:
                for t in b.r:
                    if t[0] != eng:
                        self._need(eng, t, waits)
        for b in writes:
            self._need(eng, b.w, waits)
            for t in b.r:
                self._need(eng, t, waits)
        return waits

    def op(self, eng, fn, reads=(), writes=(), signal=True):
        rec = _Rec()
        fn(rec)
        fn = rec.call
        if self.sem[eng] is None or self.cnt[eng] >= self.CH:
            self.sem[eng] = self._newsem("e_" + eng)
            self.cnt[eng] = 0
        tok = (eng, self.sem[eng], self.cnt[eng] + 1)
        waits = self._deps(eng, reads, writes)
        if signal:
            self.cnt[eng] += 1
        self.q[eng].append((waits, fn, self.sem[eng] if signal else None, 1))
        for b in reads:
            b.r.append(tok)
        for b in writes:
            b.w = tok
            b.r = []
        return tok

    def dma(self, eng, out, in_, reads=(), writes=(), is_output=False, **kw):
        r = self.ring[eng]
        i = self.ringn[eng] % self.ringsz
        self.ringn[eng] += 1
        if i >= len(r):
            r.append([self._newsem("d_" + eng), 0])
        sem, k = r[i]
        waits = self._deps(eng, reads, writes)
        if k > 0:
            self._need(eng, (None, sem, 16 * k), waits)
        r[i][1] = k + 1
        tok = (None, sem, 16 * (k + 1))
        self.q[eng].append((waits, lambda e: e.dma_start(out=out, in_=in_, **kw), sem, 16))
        for b in reads:
            b.r.append(tok)
        for b in writes:
            b.w = tok
            b.r = []
        if is_output:
            self.out_tokens.append(tok)
        return tok

    def fence(self, engs=("pe", "act", "dve", "pool")):
        toks = []
        for e in engs:
            if self.sem[e] is None or self.cnt[e] == 0:
                continue
            toks.append((e, self.sem[e], self.cnt[e]))
        for e in engs:
            waits = []
            for t in toks:
                if t[0] != e:
                    self._need(e, t, waits)
            for sem, k in self.ring["pool"]:
                if k > 0:
                    self._need(e, (None, sem, 16 * k), waits)
            if waits:
                self.q[e].append((waits, None, None, 0))

    def emit(self, block):
        nc = self.nc
        fin = []
        for t in self.out_tokens:
            self._need("pool", t, fin)
        if fin:
            self.q["pool"].append((fin, None, None, 0))

        def run(eng_name):
            def body(e):
                for waits, fn, sem, inc in self.q[eng_name]:
                    for (s, v) in waits:
                        e.wait_ge(s, v)
                    if fn is not None:
                        if callable(fn):
                            ins = fn(e)
                        else:
                            ins = getattr(e, fn[0])(*fn[1], **fn[2])
                        if sem is not None:
                            ins.then_inc(sem, inc)
            return body
        block.tensor(run("pe"))
        block.scalar(run("act"))
        block.vector(run("dve"))
        block.gpsimd(run("pool"))
        block.sync(run("sp"))


def bc(ap, n):
    return AP(ap.tensor, ap.offset, [list(x) for x in ap.ap] + [[0, n]])


class _Stop(Exception):
    pass


def build(NT=8, dbg=False, stop_after=None):
    def stage(name):
        if stop_after == name:
            raise _Stop()
    nc = bass.Bass("TRN2", target_bir_lowering=False)
    x_d = nc.dram_tensor("x", [SEQ, D], F32, kind="ExternalInput").ap()
    pos_d = nc.dram_tensor("pos", [1, SEQ], I32, kind="ExternalInput").ap()
    win_d = nc.dram_tensor("w_in", [D, NPROJ], F32, kind="ExternalInput").ap()
    wout_d = nc.dram_tensor("w_out", [DIN, D], F32, kind="ExternalInput").ap()
    kvw_d = nc.dram_tensor("kv_w", [D, 512], F32, kind="ExternalInput").ap()
    bwin_d = nc.dram_tensor("bw_in", [D, 4096], F32, kind="ExternalInput").ap()
    bwout_d = nc.dram_tensor("bw_out", [D, D], F32, kind="ExternalInput").ap()
    par_d = nc.dram_tensor("par", [128, NPAR], F32, kind="ExternalInput").ap()
    cst_d = nc.dram_tensor("cst", [128, NCST], F32, kind="ExternalInput").ap()
    lng_d = nc.dram_tensor("lng", [4, 128, D], F32, kind="ExternalInput").ap()
    out_d = nc.dram_tensor("out", [SEQ, D], F32, kind="ExternalOutput").ap()
    if dbg:
        dbg_d = nc.dram_tensor("dbg", [NT * T, D], F32, kind="ExternalOutput").ap()
    wb_d = nc.dram_tensor("wb", [NSLAB, 128, 16, 512], BF).ap()
    wbdt_d = nc.dram_tensor("wbdt", [128, 16, 64], BF).ap()
    wbv_d = nc.dram_tensor("wbv", [128, 16, 256], BF).ap()

    es = ExitStack()
    with es:
        def sb(name, shape, dt):
            return es.enter_context(nc.sbuf_tensor(name, shape, dt))
        S = Sched(nc, es)
        cst = sb("cst_sb", [128, NCST], F32)
        par = sb("par_sb", [128, NPAR], F32)
        identb = sb("identb", [128, 128], BF)
        negm4 = sb("negm4", [128, 512], BF)
        negmp4 = sb("negmp4", [128, 512], BF)
        pmatb = sb("pmatb", [128, 128], BF)
        oneE = sb("oneE", [128, 128], BF)
        oneO = sb("oneO", [128, 128], BF)
        Abc = sb("Abc", [128, 64], F32)
        expsink = sb("expsink", [128, 16], F32)
        lng = sb("lng_sb", [128, 2, D], F32)
        NSLOT = 2
        wslot = [sb(f"wslot{i}", [128, 16, 512], BF) for i in range(NSLOT)]
        wdt = sb("wdt", [128, 16, 64], BF)
        xT = sb("xT", [128, 16, T], BF)
        xin = sb("xin", [128, D], F32)
        Sst = sb("Sst", [128, 8, 512], F32)
        Sbf4 = [sb(f"Sbf{i}", [128, 512], BF) for i in range(4)]
        halo = sb("halo", [128, 48, 3], F32)
        Kcar = sb("Kcar", [128, 4, 128], BF)
        Vcar = sb("Vcar", [128, 4, 2, 128], BF)
        small = sb("small", [128, 16], F32)
        bnst = sb("bnst", [128, 4, 6], F32)
        ARENA = 96 * 1024
        arena = sb("arena", [128, ARENA // 2], BF)

        def carve(off, shape, dt):
            n = int(np.prod(shape[1:]))
            if dt == F32:
                assert off % 4 == 0
                v = arena[:, off // 2: off // 2 + 2 * n].bitcast(F32)
            else:
                v = arena[:, off // 2: off // 2 + n]
            if len(shape) == 3:
                v = v.rearrange("p (a b) -> p a b", a=shape[1])
            elif len(shape) == 4:
                v = v.rearrange("p (a b c) -> p a b c", a=shape[1], b=shape[2])
            return v
        K = 1024
        yT = carve(0, [128, 32, T], BF)
        oT = carve(0, [128, 16, T], BF)
        qT = carve(16 * K, [128, 4, T], BF)
        gT = carve(20 * K, [128, 4, T], BF)
        PT = [[carve(24 * K + (e * 2 + kb) * K, [128, 512], BF) for kb in range(2)] for e in range(2)]
        PT2 = [[carve(28 * K + (e * 2 + kb) * K, [128, 512], BF) for kb in range(2)] for e in range(2)]
        xres = carve(32 * K, [128, 4, D], F32)
        o = 32 * K
        xdt = carve(o, [128, 4, 512], BF); o += 4 * K
        xsD = carve(o, [128, 4, 512], BF); o += 4 * K
        siluz = carve(o, [128, 4, 512], BF); o += 4 * K
        MT = [carve(o + h * K, [128, 512], BF) for h in range(8)]; o += 8 * K
        LT = [carve(o + i * K, [128, 512], BF) for i in range(2)]; o += 2 * K
        CBT = carve(o, [128, 512], BF); o += K
        Btok = carve(o, [128, 4, 128], BF); o += K
        xdtw = carve(o, [128, 512], BF); o += K
        yn2 = [carve(o, [128, 512], BF), carve(o + K, [128, 512], BF)]; o += 2 * K
        ybuf = [carve(o, [128, 512], F32), carve(o + 2 * K, [128, 512], F32)]; o += 4 * K
        assert o <= 64 * K, o
        o = 64 * K
        BCT = carve(o, [128, 8, T], BF); o += 8 * K
        xsT2 = [carve(o, [128, 4, T], BF), carve(o + 4 * K, [128, 4, T], BF)]; o += 8 * K
        xpre2 = [carve(o, [128, 516], F32), carve(o + 2 * K + 16, [128, 516], F32)]; o += 4 * K + 32
        acc2 = [carve(o, [128, 512], F32), carve(o + 2 * K, [128, 512], F32)]; o += 4 * K
        dt_tok = carve(o, [128, 4, 64], F32); o += K
        negcum = carve(o, [128, 4, 64], F32); o += K
        expcum = carve(o, [128, 4, 64], F32); o += K
        dst = carve(o, [128, 4, 64], F32); o += K
        etot = carve(o, [128, 4, 64], F32); o += K
        cumT = carve(o, [128, 512], F32); o += 2 * K
        da = carve(o, [128, 64], F32); o += 256
        dtmp = carve(o, [128, 64], F32); o += 256
        assert o <= 96 * K, o
        o = 64 * K
        KT = carve(o, [128, 4, 640], BF); o += 5 * K
        Vp = carve(o, [128, 5, 4, 256], BF); o += 10 * K
        cosF = carve(o, [128, 512], F32); o += 2 * K
        sinF = carve(o, [128, 512], F32); o += 2 * K
        posi = carve(o, [128, 512], I32 if False else F32).bitcast(I32); o += 2 * K
        ang = carve(o, [128, 512], F32); o += 2 * K
        rtmp = carve(o, [128, 512], F32); o += 2 * K
        ktmp = posi
        qraw = carve(o, [128, 512], BF); o += K
        qa = carve(o, [128, 512], F32); o += 2 * K
        den = carve(o, [128, 512], F32); o += 2 * K
        attn = carve(o, [128, 512], F32); o += 2 * K
        assert o <= 96 * K, o
        xstage = sb("xstage", [128, 512], F32)

        ps = [es.enter_context(nc.psum_tensor(f"ps{i}", [128, 512], F32)) for i in range(8)]
        pb = [Buf(f"ps{i}", excl=True) for i in range(8)]

        b_cst, b_par, b_const2 = Buf(), Buf(), Buf()
        b_slab = [Buf(f"slab{i}") for i in range(NSLAB)]
        b_wbdt, b_wbv = Buf(), Buf()
        b_slot = [Buf(f"slot{i}") for i in range(NSLOT)]
        b_wdt, b_wv = Buf(), Buf()
        b_xT, b_xin, b_xstage = Buf("xT"), Buf("xin"), Buf("xstage")
        b_S = [Buf(f"S{g}") for g in range(8)]
        b_Sbf4 = [Buf() for _ in range(4)]
        b_halo = [Buf() for _ in range(48)]
        b_lng = [Buf(), Buf()]
        b_small = Buf()
        b_car = Buf()
        b_x1d = Buf()
        slot_rr = [0]

        S.dma("pool", cst[:], cst_d, writes=[b_cst])
        S.dma("pool", par[:], par_d, writes=[b_par])
        cb = [b_cst, b_par]
        S.op("dve", lambda e: e.tensor_copy(out=identb[:], in_=cst[:, C_ID:C_ID + 128]), reads=cb, writes=[b_const2])
        for j in range(4):
            S.op("dve", lambda e, j=j: e.tensor_copy(out=negm4[:, 128 * j:128 * j + 128], in_=cst[:, C_NEGM:C_NEGM + 128]), reads=cb, writes=[b_const2])
            S.op("dve", lambda e, j=j: e.tensor_copy(out=negmp4[:, 128 * j:128 * j + 128], in_=cst[:, C_NEGMP:C_NEGMP + 128]), reads=cb, writes=[b_const2])
        S.op("dve", lambda e: e.tensor_copy(out=pmatb[:], in_=cst[:, C_PMAT:C_PMAT + 128]), reads=cb, writes=[b_const2])
        S.op("dve", lambda e: e.tensor_copy(out=oneE[:], in_=cst[:, C_ONE_E:C_ONE_E + 128]), reads=cb, writes=[b_const2])
        S.op("dve", lambda e: e.tensor_copy(out=oneO[:], in_=cst[:, C_ONE_O:C_ONE_O + 128]), reads=cb, writes=[b_const2])
        S.op("act", lambda e: e.activation(out=Abc[:], in_=par[:, P_ALOG:P_ALOG + 64], func=AF.Exp), reads=cb, writes=[b_const2])
        S.op("act", lambda e: e.mul(out=Abc[:], in_=Abc[:], mul=-1.0), reads=cb, writes=[b_const2])
        S.op("act", lambda e: e.activation(out=expsink[:], in_=par[:, P_SINK:P_SINK + 16], func=AF.Exp), reads=cb, writes=[b_const2])
        S.op("pool", lambda e: e.memset(small[:, 8:9], EPS), writes=[b_small])
        S.op("pool", lambda e: e.memset(small[:, 9:10], -0.5), writes=[b_small])
        S.op("pool", lambda e: e.memset(halo[:], 0.0), writes=b_halo)
        S.op("pool", lambda e: e.memset(Sst[:], 0.0), writes=b_S)
        S.op("pool", lambda e: e.memset(Kcar[:], 0.0), writes=[b_car])
        S.op("pool", lambda e: e.memset(Vcar[:], 0.0), writes=[b_car])
        cb = [b_cst, b_par, b_const2]

        conv_src = {}
        conv_src[SL_B] = (win_d, 0, 8192); conv_src[SL_B + 1] = (win_d, 0, 8192 + 512)
        conv_src[SL_C] = (win_d, 0, 9216); conv_src[SL_C + 1] = (win_d, 0, 9216 + 512)
        for g in range(8):
            conv_src[SL_XS + g] = (win_d, 0, 4096 + 512 * g)
            conv_src[SL_Z + g] = (win_d, 0, 512 * g)
        for cs in range(4):
            for kh in range(2):
                conv_src[SL_WO + cs * 2 + kh] = (wout_d, 2048 * kh, 512 * cs)
        for g in range(4):
            conv_src[SL_Q + g] = (bwin_d, 0, 512 * g)
            conv_src[SL_G + g] = (bwin_d, 0, 2048 + 512 * g)
        for cs in range(4):
            conv_src[SL_O2 + cs] = (bwout_d, 0, 512 * cs)
        use_order = [SL_B, SL_C, SL_XS]
        for g in range(8):
            if g == 4:
                use_order += [SL_B + 1, SL_C + 1]
            if g + 1 < 8:
                use_order.append(SL_XS + g + 1)
            use_order.append(SL_Z + g)
        use_order += [SL_WO + i for i in range(8)] + [SL_K, "V"]
        for g in range(4):
            use_order += [SL_Q + g, SL_G + g]
        use_order += [SL_O2 + i for i in range(4)]
        conv_done = set()
        conv_ptr = [0]
        LOOKAHEAD = 8

        def do_convert(idx):
            if idx in conv_done:
                return
            conv_done.add(idx)
            if idx == "V":
                S.dma("pool", wbv_d, kvw_d[:, 256:512].rearrange("(kc p) c -> p kc c", p=128), writes=[b_wbv])
            elif idx == SL_K:
                for g in range(4):
                    ksrc = kvw_d[:, 64 * g:64 * g + 64].rearrange("(kc p) d -> p kc d", p=128)
                    for e2 in range(2):
                        S.dma("pool", wb_d[SL_K][:, :, 128 * g + 64 * e2:128 * g + 64 * e2 + 64], ksrc, writes=[b_slab[SL_K]])
            else:
                src, r0, c0 = conv_src[idx]
                v = src[r0:r0 + 2048, c0:c0 + 512].rearrange("(kc p) c -> p kc c", p=128)
                S.dma("pool", wb_d[idx], v, writes=[b_slab[idx]])

        def ensure_converted(idx):
            if idx not in conv_done:
                while conv_ptr[0] < len(use_order):
                    j = use_order[conv_ptr[0]]
                    conv_ptr[0] += 1
                    do_convert(j)
                    if j == idx:
                        break
                do_convert(idx)
            k = 0
            while conv_ptr[0] < len(use_order) and k < LOOKAHEAD:
                do_convert(use_order[conv_ptr[0]])
                conv_ptr[0] += 1
                k += 1

        S.dma("pool", wbdt_d, win_d[:, 10240:10304].rearrange("(kc p) c -> p kc c", p=128), writes=[b_wbdt])
        S.dma("sp", wdt[:], wbdt_d, reads=[b_wbdt], writes=[b_wdt])

        def load_slab(idx):
            ensure_converted(idx)
            i = slot_rr[0] % NSLOT
            slot_rr[0] += 1
            S.dma("sp", wslot[i][:], wb_d[idx], reads=[b_slab[idx]], writes=[b_slot[i]])
            return wslot[i], b_slot[i]

        evac_rr = [0]

        def mm_group(pbank, pbuf, mms, reads):
            n = len(mms)
            for i, (o_, l_, r_, st, sp_) in enumerate(mms):
                S.op("pe", lambda e, o_=o_, l_=l_, r_=r_, st=st, sp_=sp_: e.matmul(o_, lhsT=l_, rhs=r_, start=st, stop=sp_),
                     reads=reads, writes=[pbuf], signal=(i == n - 1))

        pj_rr = [0]

        pj_pending = {}

        def next_pj():
            i = pj_rr[0] % 2
            pj_rr[0] += 1
            if i in pj_pending:
                if (1 - i) not in pj_pending:
                    i = 1 - i
                    pj_rr[0] += 1
                else:
                    pj_pending.pop(i)()
            return ps[i], pb[i]

        try:
            for t in range(NT):
                tok0 = t * T
                bA = {n: Buf(n) for n in ["BCT", "xsT", "xdt", "xsD", "xstok", "siluz", "CBT", "Btok", "xdtw", "yn", "sqj",
                                          "toff", "yv", "y2", "xpre", "acc", "dtv", "cumT", "da", "dtmp", "yT", "LT0", "LT1"]}
                bMT = [Buf(f"MT{h}") for h in range(8)]
                bxsT = [Buf("xsT0"), Buf("xsT1")]
                bybuf = [Buf("yb0"), Buf("yb1")]
                byn2 = [Buf("yn0"), Buf("yn1")]
                bsmall2 = [Buf("sm0"), Buf("sm1")]
                bxp2 = [Buf("xp0"), Buf("xp1")]
                bac2 = [Buf("ac0"), Buf("ac1")]
                conv_rr = [0]
                S.dma("pool", lng[:, 0, :], lng_d[0], writes=[b_lng[0]])
                S.dma("pool", lng[:, 1, :], lng_d[1], writes=[b_lng[1]])
                pre_BC = [load_slab(SL_B), load_slab(SL_C)]
                for tb in range(4):
                    S.dma("sp", xin[:], x_d[tok0 + 128 * tb: tok0 + 128 * tb + 128, :], writes=[b_xin])
                    for q4 in range(4):
                        pbank, pbuf = ps[2 + (q4 % 2)], pb[2 + (q4 % 2)]
                        mm_group(pbank, pbuf, [(pbank[:, 128 * j:128 * j + 128], xin[:, (4 * q4 + j) * 128:(4 * q4 + j) * 128 + 128],
                                                cst[:, C_ID:C_ID + 128], True, True) for j in range(4)], reads=[b_xin] + cb)
                        eng = "act" if (q4 % 2) else "dve"
                        outap = xT[:, 4 * q4:4 * q4 + 4, 128 * tb:128 * tb + 128]
                        inap = pbank[:].rearrange("p (a b) -> p a b", a=4)
                        if eng == "act":
                            S.op("act", lambda e, o_=outap, i_=inap: e.activation(out=o_, in_=i_, func=AF.Copy), reads=[pbuf], writes=[b_xT])
                        else:
                            S.op("dve", lambda e, o_=outap, i_=inap: e.tensor_copy(out=o_, in_=i_), reads=[pbuf], writes=[b_xT])
                stage('A1')
                for tb in range(4):
                    pbank, pbuf = ps[3], pb[3]
                    mm_group(pbank, pbuf, [(pbank[:, 0:64], xT[:, kc, 128 * tb:128 * tb + 128], wdt[:, kc, :], kc == 0, kc == 15)
                                           for kc in range(16)], reads=[b_xT, b_wdt])
                    S.op("dve", lambda e: e.tensor_tensor(out=dtmp[:], in0=pbank[:, 0:64], in1=par[:, P_DTB:P_DTB + 64], op=ALU.add),
                         reads=[pbuf] + cb, writes=[bA["dtmp"]])
                    S.op("act", lambda e: e.activation(out=dtmp[:], in_=dtmp[:], func=AF.Exp), reads=[bA["dtmp"]], writes=[bA["dtmp"]])
                    S.op("act", lambda e, tb=tb: e.activation(out=dt_tok[:, tb, :], in_=dtmp[:], func=AF.Ln, bias=1.0),
                         reads=[bA["dtmp"]], writes=[bA["dtv"]])
                    S.op("dve", lambda e, tb=tb: e.tensor_tensor(out=da[:], in0=dt_tok[:, tb, :], in1=Abc[:], op=ALU.mult),
                         reads=[bA["dtv"]] + cb, writes=[bA["da"]])
                    mm_group(pbank, pbuf, [(pbank[:, 0:64], cst[:, C_U:C_U + 128], da[:], True, True),
                                           (pbank[:, 64:128], cst[:, C_ONES:C_ONES + 128], da[:], True, True),
                                           (pbank[0:64, 128:256], da[:], cst[:, C_U:C_U + 128], True, True)],
                             reads=[bA["da"]] + cb)
                    S.op("act", lambda e, tb=tb: e.mul(out=negcum[:, tb, :], in_=pbank[:, 0:64], mul=-1.0), reads=[pbuf], writes=[bA["dtv"]])
                    S.op("act", lambda e, tb=tb: e.activation(out=expcum[:, tb, :], in_=pbank[:, 0:64], func=AF.Exp), reads=[pbuf], writes=[bA["dtv"]])
                    S.op("act", lambda e, tb=tb: e.activation(out=etot[:, tb, :], in_=pbank[:, 64:128], func=AF.Exp), reads=[pbuf], writes=[bA["dtv"]])
                    S.op("dve", lambda e, tb=tb: e.tensor_tensor(out=dtmp[:], in0=pbank[:, 64:128], in1=negcum[:, tb, :], op=ALU.add),
                         reads=[pbuf, bA["dtv"]], writes=[bA["dtmp"]])
                    S.op("act", lambda e, tb=tb: e.activation(out=dst[:, tb, :], in_=dtmp[:], func=AF.Exp), reads=[bA["dtmp"]], writes=[bA["dtv"]])
                    S.op("dve", lambda e, tb=tb: e.tensor_copy(out=cumT[0:64, 128 * tb:128 * tb + 128], in_=pbank[0:64, 128:256]),
                         reads=[pbuf], writes=[bA["cumT"]])

                stage('A2')

                def proj_fm_conv(slot, bslot, fc, ch, dest, bdest, defer=False):
                    pbank, pbuf = next_pj()
                    ci = conv_rr[0] % 2
                    conv_rr[0] += 1
                    xpre, acc = xpre2[ci], acc2[ci]
                    bxpre, bacc = bxp2[ci], bac2[ci]
                    mm_group(pbank, pbuf, [(pbank[:], slot[:, kc, 128 * fc:128 * fc + 128], xT[:, kc, :], kc == 0, kc == 15)
                                           for kc in range(16)], reads=[bslot, b_xT])

                    bank_i = 0 if pbank is ps[0] else 1
                    done = [False]

                    def evac():
                        if done[0]:
                            return
                        done[0] = True
                        pj_pending.pop(bank_i, None)
                        proj_fm_conv_ev(pbank, pbuf, xpre, acc, bxpre, bacc, ch, dest, bdest)
                    if defer:
                        pj_pending[bank_i] = evac
                        return evac
                    evac()

                def proj_fm_conv_ev(pbank, pbuf, xpre, acc, bxpre, bacc, ch, dest, bdest):
                    S.op("pool", lambda e: e.tensor_copy(out=xpre[:, 0:3], in_=halo[:, ch, :]), reads=[b_halo[ch]], writes=[bxpre])
                    S.op("act", lambda e: e.activation(out=xpre[:, 3:515], in_=pbank[:], func=AF.Copy), reads=[pbuf], writes=[bxpre])
                    S.op("pool", lambda e: e.tensor_copy(out=halo[:, ch, :], in_=xpre[:, 512:515]), reads=[bxpre], writes=[b_halo[ch]])
                    S.op("act", lambda e: e.activation(out=acc[:], in_=pbank[:], func=AF.Identity,
                                                       bias=par[:, P_CB + ch:P_CB + ch + 1], scale=par[:, P_CW + 4 * ch + 3:P_CW + 4 * ch + 4]),
                         reads=[pbuf] + cb, writes=[bacc])
                    for j in (2, 1, 0):
                        S.op("dve", lambda e, j=j: e.scalar_tensor_tensor(out=acc[:], in0=xpre[:, j:j + 512],
                                                                          scalar=par[:, P_CW + 4 * ch + j:P_CW + 4 * ch + j + 1],
                                                                          in1=acc[:], op0=ALU.mult, op1=ALU.add),
                             reads=[bxpre, bacc] + cb, writes=[bacc])
                    S.op("act", lambda e: e.activation(out=dest, in_=acc[:], func=AF.Silu), reads=[bacc], writes=[bdest])

                xs_state = {}

                def xs_load(g):
                    xs_state[g] = load_slab(SL_XS + g)

                def xs_proj_chunk(g, fc, defer=False):
                    slot, bslot = xs_state[g]
                    return proj_fm_conv(slot, bslot, fc, 4 * g + fc, xsT2[g % 2][:, fc, :], bxsT[g % 2], defer=defer)

                for half in range(2):
                    slot, bslot = pre_BC[0] if half == 0 else load_slab(SL_B + half)
                    for fc in range(4):
                        proj_fm_conv(slot, bslot, fc, 32 + 4 * half + fc, BCT[:, fc, :], bA["BCT"])
                    slot, bslot = pre_BC[1] if half == 0 else load_slab(SL_C + half)
                    for fc in range(4):
                        proj_fm_conv(slot, bslot, fc, 40 + 4 * half + fc, BCT[:, 4 + fc, :], bA["BCT"])
                    if half == 0:
                        xs_load(0)
                        for fc in range(4):
                            xs_proj_chunk(0, fc)
                        if t > 0:
                            S.fence()
                    for gl in range(4):
                        g = 4 * half + gl
                        BTg = BCT[:, gl, :]
                        CTg = BCT[:, 4 + gl, :]
                        xsT = xsT2[g % 2]
                        bxs = bxsT[g % 2]
                        if g + 1 < 8:
                            xs_load(g + 1)
                        S.op("act", lambda e, g=g: e.activation(out=Sbf4[0][:], in_=Sst[:, g, :], func=AF.Copy), reads=[b_S[g]], writes=[b_Sbf4[0]])
                        for c in range(4):
                            pbank, pbuf = ps[2 + c % 2], pb[2 + c % 2]
                            mm_group(pbank, pbuf, [(pbank[:, 128 * fc:128 * fc + 128], xsT[:, fc, 128 * c:128 * c + 128], identb[:], True, True)
                                                   for fc in range(4)], reads=[bxs] + cb)
                            p3 = pbank[:].rearrange("p (h d) -> p h d", d=64)
                            S.op("dve", lambda e, c=c, g=g, p3=p3: e.tensor_tensor(out=xdt[:, c, :].rearrange("p (h d) -> p h d", d=64), in0=p3,
                                                                                   in1=bc(dt_tok[:, c, 8 * g:8 * g + 8], 64), op=ALU.mult),
                                 reads=[pbuf, bA["dtv"]], writes=[bA["xdt"]])
                            S.op("dve", lambda e, c=c, g=g, p3=p3: e.tensor_tensor(out=xsD[:, c, :].rearrange("p (h d) -> p h d", d=64), in0=p3,
                                                                                   in1=bc(par[:, P_AD + 8 * g:P_AD + 8 * g + 8], 64), op=ALU.mult),
                                 reads=[pbuf] + cb, writes=[bA["xsD"]])
                        pbank, pbuf = ps[2], pb[2]
                        mm_group(pbank, pbuf, [(pbank[:, 128 * c:128 * c + 128], BTg[:, 128 * c:128 * c + 128], identb[:], True, True)
                                               for c in range(4)], reads=[bA["BCT"]] + cb)
                        S.op("act", lambda e, pbank=pbank: e.activation(out=Btok[:].rearrange("p a b -> p (a b)"), in_=pbank[:], func=AF.Copy),
                             reads=[pbuf], writes=[bA["Btok"]])
                        pbank, pbuf = ps[3], pb[3]
                        mm_group(pbank, pbuf, [(pbank[:, 128 * c:128 * c + 128], BTg[:, 128 * c:128 * c + 128], CTg[:, 128 * c:128 * c + 128], True, True)
                                               for c in range(4)], reads=[bA["BCT"]])
                        S.op("act", lambda e, pbank=pbank: e.activation(out=CBT[:], in_=pbank[:], func=AF.Copy), reads=[pbuf], writes=[bA["CBT"]])
                        zslot = load_slab(SL_Z + g)

                        def FILLZ(tb, zslot=zslot):
                            slot, bslot = zslot
                            pbank, pbuf = next_pj()
                            mm_group(pbank, pbuf, [(pbank[:], xT[:, kc, 128 * tb:128 * tb + 128], slot[:, kc, :], kc == 0, kc == 15)
                                                   for kc in range(16)], reads=[bslot, b_xT])
                            S.op("act", lambda e: e.activation(out=siluz[:, tb, :], in_=pbank[:], func=AF.Silu),
                                 reads=[pbuf], writes=[bA["siluz"]])

                        def state_step(c, g=g):
                            S.op("dve", lambda e: e.tensor_tensor(out=xdtw[:].rearrange("p (h d) -> p h d", d=64),
                                                                            in0=xdt[:, c, :].rearrange("p (h d) -> p h d", d=64),
                                                                            in1=bc(dst[:, c, 8 * g:8 * g + 8], 64), op=ALU.mult),
                                 reads=[bA["xdt"], bA["dtv"]], writes=[bA["xdtw"]])
                            pS, bpS = ps[6 + c % 2], pb[6 + c % 2]
                            mm_group(pS, bpS, [(pS[:], Btok[:, c, :], xdtw[:], True, True)], reads=[bA["Btok"], bA["xdtw"]])
                            S.op("dve", lambda e: e.tensor_tensor(out=Sst[:, g, :].rearrange("p (h d) -> p h d", d=64),
                                                                            in0=Sst[:, g, :].rearrange("p (h d) -> p h d", d=64),
                                                                            in1=bc(etot[:, c, 8 * g:8 * g + 8], 64), op=ALU.mult),
                                 reads=[b_S[g], bA["dtv"]], writes=[b_S[g]])
                            S.op("dve", lambda e: e.tensor_tensor(out=Sst[:, g, :], in0=pS[:], in1=Sst[:, g, :], op=ALU.add),
                                 reads=[bpS, b_S[g]], writes=[b_S[g]])
                            if c < 3:
                                S.op("act", lambda e: e.activation(out=Sbf4[c + 1][:], in_=Sst[:, g, :], func=AF.Copy),
                                     reads=[b_S[g]], writes=[b_Sbf4[c + 1]])
                        state_step(0); FILLZ(0); state_step(1); state_step(2); FILLZ(1); state_step(3)
                        for h in range(8):
                            hh = 8 * g + h
                            pbank, pbuf = ps[4 + (h % 2)], pb[4 + (h % 2)]
                            sel = cst[0:64, C_ID + hh:C_ID + hh + 1].broadcast_to([64, 128])
                            mm_group(pbank, pbuf, [(pbank[:], sel, cumT[0:64, :], True, False),
                                                   (pbank[:], identb[:], negm4[:], False, True)], reads=[bA["cumT"]] + cb)
                            lt, blt = LT[h % 2], bA[f"LT{h % 2}"]
                            for c in range(4):
                                S.op("act", lambda e, c=c, hh=hh, lt=lt, pbank=pbank: e.activation(
                                    out=lt[:, 128 * c:128 * c + 128], in_=pbank[:, 128 * c:128 * c + 128], func=AF.Exp,
                                    bias=negcum[:, c, hh:hh + 1], scale=1.0), reads=[pbuf, bA["dtv"]], writes=[blt])
                            S.op("dve", lambda e, h=h, lt=lt: e.tensor_tensor(out=MT[h][:], in0=lt[:], in1=CBT[:], op=ALU.mult),
                                 reads=[blt, bA["CBT"]], writes=[bMT[h]])
                        def Y1(c, g=g, CTg=CTg):
                            cs_ = slice(128 * c, 128 * c + 128)
                            yb, byb = ybuf[c % 2], bybuf[c % 2]
                            pA, bpA = (ps[6], pb[6]) if c % 2 == 0 else (ps[3], pb[3])
                            pB, bpB = (ps[7], pb[7]) if c % 2 == 0 else (ps[4], pb[4])
                            mms = [(pA[:], identb[:], xsD[:, c, :], True, False)]
                            for h in range(8):
                                mms.append((pA[:, 64 * h:64 * h + 64], MT[h][:, cs_], xdt[:, c, 64 * h:64 * h + 64], False, h == 7))
                            mm_group(pA, bpA, mms, reads=[bA["xsD"], bA["xdt"]] + bMT + cb)
                            mm_group(pB, bpB, [(pB[:], CTg[:, cs_], Sbf4[c][:], True, True)], reads=[bA["BCT"], b_Sbf4[c]])
                            S.op("dve", lambda e: e.tensor_tensor(out=yb[:].rearrange("p (h d) -> p h d", d=64),
                                                                  in0=pB[:].rearrange("p (h d) -> p h d", d=64),
                                                                  in1=bc(expcum[:, c, 8 * g:8 * g + 8], 64), op=ALU.mult),
                                 reads=[bpB, bA["dtv"]], writes=[byb])
                            S.op("dve", lambda e: e.tensor_tensor(out=yb[:], in0=pA[:], in1=yb[:], op=ALU.add),
                                 reads=[bpA, byb], writes=[byb])
                            S.op("dve", lambda e: e.tensor_tensor(out=yb[:], in0=yb[:], in1=siluz[:, c, :], op=ALU.mult),
                                 reads=[byb, bA["siluz"]], writes=[byb])

                        def S2(c):
                            yb, byb = ybuf[c % 2], bybuf[c % 2]
                            ynb, bynb = yn2[c % 2], byn2[c % 2]
                            sm, bsm = small[:, 10 + 3 * (c % 2):13 + 3 * (c % 2)], bsmall2[c % 2]
                            S.op("dve", lambda e: e.scalar_tensor_tensor(out=ynb[:], in0=yb[:], scalar=1.0, in1=yb[:], op0=ALU.mult, op1=ALU.mult,
                                                                         accum_out=sm[:, 0:1]), reads=[byb], writes=[bynb, bsm])
                            S.op("dve", lambda e: e.tensor_scalar(out=sm[:, 1:2], in0=sm[:, 0:1], scalar1=1.0 / 512.0, scalar2=EPS, op0=ALU.mult, op1=ALU.add),
                                 reads=[bsm], writes=[bsm])
                            S.op("pool", lambda e: e.tensor_tensor(out=sm[:, 2:3], in0=sm[:, 1:2], in1=small[:, 9:10], op=ALU.pow),
                                 reads=[bsm, b_small], writes=[bsm])
                            S.op("act", lambda e: e.activation(out=ynb[:], in_=yb[:], func=AF.Identity, scale=sm[:, 2:3]),
                                 reads=[byb, bsm], writes=[bynb])

                        def S3(c, g=g):
                            cs_ = slice(128 * c, 128 * c + 128)
                            ynb, bynb = yn2[c % 2], byn2[c % 2]
                            pT2, bpT2 = (ps[2], pb[2]) if c % 2 == 0 else (ps[5], pb[5])
                            mm_group(pT2, bpT2, [(pT2[:, 128 * fc:128 * fc + 128], ynb[:, 128 * fc:128 * fc + 128], identb[:], True, True)
                                                 for fc in range(4)], reads=[bynb] + cb)
                            for fc in range(4):
                                nwc = par[:, P_NW + 4 * g + fc:P_NW + 4 * g + fc + 1]
                                if fc < 2:
                                    S.op("act", lambda e, fc=fc: e.activation(out=yT[:, 4 * g + fc, cs_], in_=pT2[:, 128 * fc:128 * fc + 128],
                                                                              func=AF.Identity, scale=nwc), reads=[bpT2] + cb, writes=[bA["yT"]])
                                else:
                                    S.op("dve", lambda e, fc=fc: e.tensor_scalar(out=yT[:, 4 * g + fc, cs_], in0=pT2[:, 128 * fc:128 * fc + 128],
                                                                                 scalar1=nwc, scalar2=None, op0=ALU.mult), reads=[bpT2] + cb, writes=[bA["yT"]])

                        fill_ev = {}

                        def FILL(c, g=g):
                            if g + 1 < 8:
                                fill_ev[c] = xs_proj_chunk(g + 1, c, defer=True)

                        def FILLEV(c):
                            if c in fill_ev:
                                fill_ev.pop(c)()

                        Y1(0); S2(0)
                        Y1(1); S2(1); FILLZ(2); FILL(0); S3(0); FILLEV(0)
                        Y1(2); S2(2); FILLZ(3); FILL(1); S3(1); FILLEV(1)
                        Y1(3); S2(3); FILL(2); S3(2); FILLEV(2)
                        FILL(3); S3(3); FILLEV(3)

                stage('A')
                S.fence()
                bC = {n: Buf(n) for n in ["KT", "Vp", "rope", "qraw", "qa", "qT", "gT", "oT", "den", "attn", "tmp"]}
                bPT = [[Buf(), Buf()], [Buf(), Buf()]]
                bPT2 = [[Buf(), Buf()], [Buf(), Buf()]]
                pos_bc = AP(pos_d.tensor, tok0, [[0, 128], [1, 512]])
                S.dma("pool", posi, pos_bc, writes=[bC["rope"]])
                S.op("dve", lambda e: e.tensor_copy(out=ang[:], in_=posi), reads=[bC["rope"]], writes=[bC["rope"]])
                S.op("dve", lambda e: e.tensor_scalar(out=ang[:], in0=ang[:], scalar1=cst[:, C_FREQ:C_FREQ + 1], scalar2=None, op0=ALU.mult),
                     reads=[bC["rope"]] + cb, writes=[bC["rope"]])

                def sincos(dest, shift, scale_ap):
                    S.op("dve", lambda e: e.tensor_scalar(out=rtmp[:], in0=ang[:], scalar1=shift, scalar2=1.0 / (2 * math.pi),
                                                          op0=ALU.add, op1=ALU.mult), reads=[bC["rope"]], writes=[bC["tmp"]])
                    S.op("dve", lambda e: e.tensor_scalar(out=rtmp[:], in0=rtmp[:], scalar1=0.5, scalar2=None, op0=ALU.add),
                         reads=[bC["tmp"]], writes=[bC["tmp"]])
                    S.op("dve", lambda e: e.tensor_copy(out=ktmp, in_=rtmp[:]), reads=[bC["tmp"]], writes=[bC["tmp"]])
                    S.op("dve", lambda e: e.tensor_copy(out=rtmp[:], in_=ktmp), reads=[bC["tmp"]], writes=[bC["tmp"]])
                    S.op("dve", lambda e: e.scalar_tensor_tensor(out=rtmp[:], in0=rtmp[:], scalar=-2 * math.pi, in1=ang[:], op0=ALU.mult, op1=ALU.add),
                         reads=[bC["tmp"], bC["rope"]], writes=[bC["tmp"]])
                    S.op("dve", lambda e: e.tensor_scalar(out=rtmp[:], in0=rtmp[:], scalar1=shift, scalar2=None, op0=ALU.add),
                         reads=[bC["tmp"]], writes=[bC["tmp"]])
                    S.op("dve", lambda e: e.tensor_scalar(out=qa[:], in0=rtmp[:], scalar1=-math.pi, scalar2=2 * math.pi, op0=ALU.is_lt, op1=ALU.mult),
                         reads=[bC["tmp"]], writes=[bC["qa"]])
                    S.op("dve", lambda e: e.tensor_tensor(out=rtmp[:], in0=rtmp[:], in1=qa[:], op=ALU.add), reads=[bC["tmp"], bC["qa"]], writes=[bC["tmp"]])
                    S.op("dve", lambda e: e.tensor_scalar(out=qa[:], in0=rtmp[:], scalar1=math.pi, scalar2=-2 * math.pi, op0=ALU.is_gt, op1=ALU.mult),
                         reads=[bC["tmp"]], writes=[bC["qa"]])
                    S.op("dve", lambda e: e.tensor_tensor(out=rtmp[:], in0=rtmp[:], in1=qa[:], op=ALU.add), reads=[bC["tmp"], bC["qa"]], writes=[bC["tmp"]])
                    S.op("dve", lambda e: e.tensor_scalar(out=rtmp[:], in0=rtmp[:], scalar1=-3.1415925, scalar2=3.1415925, op0=ALU.max, op1=ALU.min),
                         reads=[bC["tmp"]], writes=[bC["tmp"]])
                    if scale_ap is None:
                        S.op("act", lambda e: e.activation(out=dest[:], in_=rtmp[:], func=AF.Sin), reads=[bC["tmp"]], writes=[bC["rope"]])
                    else:
                        S.op("act", lambda e: e.activation(out=dest[:], in_=rtmp[:], func=AF.Sin, scale=scale_ap), reads=[bC["tmp"]] + cb, writes=[bC["rope"]])
                sincos(cosF, math.pi / 2, None)
                sincos(sinF, 0.0, cst[:, C_SGN:C_SGN + 1])

                S.op("pool", lambda e: e.memset(Vp[:], 0.0), writes=[bC["Vp"]])
                S.op("pool", lambda e: e.tensor_copy(out=KT[:, :, 0:128], in_=Kcar[:]), reads=[b_car], writes=[bC["KT"]])
                S.op("pool", lambda e: e.tensor_copy(out=Vp[:, 0, :, :].rearrange("p g (v d) -> p g v d", v=2), in_=Vcar[:]), reads=[b_car], writes=[bC["Vp"]])

                b_xres = [Buf(f"xres{i}") for i in range(4)]
                for cs in range(4):
                    for kh in range(2):
                        slot, bslot = load_slab(SL_WO + cs * 2 + kh)
                        for tb in range(4):
                            pO, bpO = ps[4 + tb], pb[4 + tb]
                            mm_group(pO, bpO, [(pO[:], yT[:, 16 * kh + kc, 128 * tb:128 * tb + 128], slot[:, kc, :],
                                                (kh == 0 and kc == 0), (kh == 1 and kc == 15)) for kc in range(16)],
                                     reads=[bslot, bA["yT"]])
                    for tb in range(4):
                        pO, bpO = ps[4 + tb], pb[4 + tb]
                        r0 = tok0 + 128 * tb
                        S.dma("pool", xstage[:], x_d[r0:r0 + 128, 512 * cs:512 * cs + 512], writes=[b_xstage])
                        S.op("dve", lambda e, tb=tb, cs=cs, pO=pO: e.scalar_tensor_tensor(
                            out=xres[:, tb, 512 * cs:512 * cs + 512], in0=xstage[:], scalar=ALPHA, in1=pO[:], op0=ALU.mult, op1=ALU.add),
                            reads=[b_xstage, bpO], writes=[b_xres[tb]])

                def layer_norm(tb, li):
                    v = xres[:, tb, :]
                    for j in range(4):
                        S.op("dve", lambda e, j=j, v=v: e.bn_stats(out=bnst[:, j, :], in_=v[:, 512 * j:512 * j + 512]),
                             reads=[b_xres[tb]], writes=[b_small])
                    S.op("dve", lambda e: e.bn_aggr(out=small[:, 4:6], in_=bnst[:].rearrange("p a b -> p (a b)")), reads=[b_small], writes=[b_small])
                    S.op("dve", lambda e: e.tensor_scalar(out=small[:, 6:7], in0=small[:, 5:6], scalar1=EPS, scalar2=None, op0=ALU.add),
                         reads=[b_small], writes=[b_small])
                    S.op("pool", lambda e: e.tensor_tensor(out=small[:, 7:8], in0=small[:, 6:7], in1=small[:, 9:10], op=ALU.pow),
                         reads=[b_small], writes=[b_small])
                    S.op("dve", lambda e, v=v: e.scalar_tensor_tensor(out=v, in0=v, scalar=small[:, 4:5], in1=lng[:, 0, :],
                                                                      op0=ALU.subtract, op1=ALU.mult),
                         reads=[b_small, b_xres[tb], b_lng[0]], writes=[b_xres[tb]])
                    S.op("dve", lambda e, v=v: e.scalar_tensor_tensor(out=v, in0=v, scalar=small[:, 7:8], in1=lng[:, 1, :],
                                                                      op0=ALU.mult, op1=ALU.add),
                         reads=[b_small, b_xres[tb], b_lng[1]], writes=[b_xres[tb]])

                for tb in range(4):
                    layer_norm(tb, 0)
                    if dbg:
                        S.dma("pool", dbg_d[tok0 + 128 * tb:tok0 + 128 * tb + 128, :], xres[:, tb, :], reads=[b_xres[tb]], is_output=True)
                    for q4 in range(4):
                        pbank, pbuf = ps[2 + (q4 % 2)], pb[2 + (q4 % 2)]
                        mm_group(pbank, pbuf, [(pbank[:, 128 * j:128 * j + 128], xres[:, tb, (4 * q4 + j) * 128:(4 * q4 + j) * 128 + 128],
                                                cst[:, C_ID:C_ID + 128], True, True) for j in range(4)], reads=[b_xres[tb]] + cb)
                        outap = xT[:, 4 * q4:4 * q4 + 4, 128 * tb:128 * tb + 128]
                        inap = pbank[:].rearrange("p (a b) -> p a b", a=4)
                        if q4 % 2:
                            S.op("act", lambda e, o_=outap, i_=inap: e.activation(out=o_, in_=i_, func=AF.Copy), reads=[pbuf], writes=[b_xT])
                        else:
                            S.op("dve", lambda e, o_=outap, i_=inap: e.tensor_copy(out=o_, in_=i_), reads=[pbuf], writes=[b_xT])
                S.dma("pool", lng[:, 0, :], lng_d[2], writes=[b_lng[0]])
                S.dma("pool", lng[:, 1, :], lng_d[3], writes=[b_lng[1]])

                stage('B')
                S.fence()
                def rope_evac(pbank, pbuf, bias_ap, dest, bdest):
                    S.op("act", lambda e: e.activation(out=qraw[:], in_=pbank[:], func=AF.Identity, bias=bias_ap), reads=[pbuf] + cb, writes=[bC["qraw"]])
                    pP, bpP = ps[2], pb[2]
                    mm_group(pP, bpP, [(pP[:], pmatb[:], qraw[:], True, True)], reads=[bC["qraw"]] + cb)
                    S.op("pool", lambda e: e.tensor_tensor(out=qa[:], in0=qraw[:], in1=cosF[:], op=ALU.mult), reads=[bC["qraw"], bC["rope"]], writes=[bC["qa"]])
                    S.op("dve", lambda e: e.tensor_tensor(out=rtmp[:], in0=pP[:], in1=sinF[:], op=ALU.mult), reads=[bpP, bC["rope"]], writes=[bC["tmp"]])
                    S.op("dve", lambda e: e.tensor_tensor(out=dest, in0=rtmp[:], in1=qa[:], op=ALU.add), reads=[bC["tmp"], bC["qa"]], writes=[bdest])

                slot, bslot = load_slab(SL_K)
                for g in range(4):
                    pbank, pbuf = next_pj()
                    mm_group(pbank, pbuf, [(pbank[:], slot[:, kc, 128 * g:128 * g + 128], xT[:, kc, :], kc == 0, kc == 15) for kc in range(16)],
                             reads=[bslot, b_xT])
                    rope_evac(pbank, pbuf, par[:, P_KB + g:P_KB + g + 1], KT[:, g, 128:640], bC["KT"])
                ensure_converted("V")
                i_ = slot_rr[0] % NSLOT
                slot_rr[0] += 1
                S.dma("sp", wslot[i_][:, :, 0:256], wbv_d, reads=[b_wbv], writes=[b_slot[i_]])
                wv, b_wv = wslot[i_], b_slot[i_]
                for tb in range(4):
                    pbank, pbuf = next_pj()
                    mm_group(pbank, pbuf, [(pbank[:, 0:256], xT[:, kc, 128 * tb:128 * tb + 128], wv[:, kc, 0:256], kc == 0, kc == 15) for kc in range(16)],
                             reads=[b_wv, b_xT])
                    vv = Vp[:, 1 + tb, :, :].rearrange("p g (v d) -> p g v d", v=2)
                    pv = pbank[:, 0:256].rearrange("p (g d) -> p g d", d=64)
                    vb = par[:, P_VB:P_VB + 256].rearrange("p (g d) -> p g d", d=64)
                    S.op("dve", lambda e, vv=vv, pv=pv, vb=vb: e.tensor_tensor(out=vv[:, :, 0, 0:64], in0=pv, in1=vb, op=ALU.add),
                         reads=[pbuf] + cb, writes=[bC["Vp"]])
                    S.op("dve", lambda e, vv=vv, pv=pv, vb=vb: e.tensor_tensor(out=vv[:, :, 1, 64:128], in0=pv, in1=vb, op=ALU.add),
                         reads=[pbuf] + cb, writes=[bC["Vp"]])
                S.op("pool", lambda e: e.tensor_copy(out=Kcar[:], in_=KT[:, :, 512:640]), reads=[bC["KT"]], writes=[b_car])
                S.op("pool", lambda e: e.tensor_copy(out=Vcar[:], in_=Vp[:, 4, :, :].rearrange("p g (v d) -> p g v d", v=2)), reads=[bC["Vp"]], writes=[b_car])

                stage('C1')
                blk_i = 0
                for g in range(4):
                    slot, bslot = load_slab(SL_Q + g)
                    for fc in range(4):
                        pbank, pbuf = next_pj()
                        mm_group(pbank, pbuf, [(pbank[:], slot[:, kc, 128 * fc:128 * fc + 128], xT[:, kc, :], kc == 0, kc == 15) for kc in range(16)],
                                 reads=[bslot, b_xT])
                        rope_evac(pbank, pbuf, par[:, P_QB + 4 * g + fc:P_QB + 4 * g + fc + 1], qT[:, fc, :], bC["qT"])
                    slot, bslot = load_slab(SL_G + g)
                    for fc in range(4):
                        pbank, pbuf = next_pj()
                        mm_group(pbank, pbuf, [(pbank[:], slot[:, kc, 128 * fc:128 * fc + 128], xT[:, kc, :], kc == 0, kc == 15) for kc in range(16)],
                                 reads=[bslot, b_xT])
                        S.op("act", lambda e, fc=fc, pbank=pbank: e.activation(out=gT[:, fc, :], in_=pbank[:], func=AF.Silu), reads=[pbuf], writes=[bC["gT"]])
                    for c in range(4):
                        kbs = [0, 1] if not (t == 0 and c == 0) else [1]
                        PTs, bPTs = (PT, bPT) if (blk_i % 2 == 0) else (PT2, bPT2)
                        blk_i += 1
                        for e2 in range(2):
                            for kb in kbs:
                                pS_, bpS_ = ps[3 + (e2 * 2 + kb) % 2], pb[3 + (e2 * 2 + kb) % 2]
                                kcol = 128 * c + 128 * kb
                                mm_group(pS_, bpS_, [(pS_[:].rearrange("p (a b) -> p a b", a=4), KT[64 * e2:64 * e2 + 64, g, kcol:kcol + 128],
                                                      qT[64 * e2:64 * e2 + 64, :, 128 * c:128 * c + 128], True, False),
                                                     (pS_[:], identb[:], (negm4 if kb == 1 else negmp4)[:], False, True)],
                                         reads=[bC["KT"], bC["qT"]] + cb)
                                S.op("act", lambda e, pS_=pS_, dest=PTs[e2][kb]: e.activation(out=dest[:], in_=pS_[:], func=AF.Exp, scale=0.125),
                                     reads=[bpS_], writes=[bPTs[e2][kb]])
                        pO, bpO = ps[5], pb[5]
                        pSm, bpSm = ps[6], pb[6]
                        combos = [(e2, kb) for e2 in range(2) for kb in kbs]
                        mmo, mms_ = [], []
                        for i, (e2, kb) in enumerate(combos):
                            vblk = c + kb
                            mmo.append((pO[:], Vp[:, vblk, g, 128 * e2:128 * e2 + 128], PTs[e2][kb][:], i == 0, i == len(combos) - 1))
                            mms_.append((pSm[:], (oneE if e2 == 0 else oneO)[:], PTs[e2][kb][:], i == 0, i == len(combos) - 1))
                        rds = [bC["Vp"]] + [bPTs[e2][kb] for (e2, kb) in combos] + cb
                        mm_group(pO, bpO, mmo, reads=rds)
                        mm_group(pSm, bpSm, mms_, reads=rds)
                        S.op("dve", lambda e, g=g, pSm=pSm: e.tensor_tensor(out=den[:].rearrange("p (a b) -> p a b", a=4),
                                                                            in0=pSm[:].rearrange("p (a b) -> p a b", a=4),
                                                                            in1=bc(expsink[:, 4 * g:4 * g + 4], 128), op=ALU.add),
                             reads=[bpSm] + cb, writes=[bC["den"]])
                        S.op("dve", lambda e: e.reciprocal(out=den[:], in_=den[:]), reads=[bC["den"]], writes=[bC["den"]])
                        S.op("dve", lambda e, pO=pO: e.tensor_tensor(out=attn[:], in0=pO[:], in1=den[:], op=ALU.mult), reads=[bpO, bC["den"]], writes=[bC["attn"]])
                        S.op("pool", lambda e, g=g, c=c: e.tensor_tensor(out=oT[:, 4 * g:4 * g + 4, 128 * c:128 * c + 128],
                                                                         in0=attn[:].rearrange("p (a b) -> p a b", a=4),
                                                                         in1=gT[:, :, 128 * c:128 * c + 128], op=ALU.mult),
                             reads=[bC["attn"], bC["gT"]], writes=[bC["oT"]])
                stage('C2')
                for cs in range(4):
                    slot, bslot = load_slab(SL_O2 + cs)
                    for tb in range(4):
                        pO, bpO = ps[4 + tb], pb[4 + tb]
                        mm_group(pO, bpO, [(pO[:], oT[:, kc, 128 * tb:128 * tb + 128], slot[:, kc, :], kc == 0, kc == 15) for kc in range(16)],
                                 reads=[bslot, bC["oT"]])
                        S.op("dve", lambda e, tb=tb, cs=cs, pO=pO: e.scalar_tensor_tensor(
                            out=xres[:, tb, 512 * cs:512 * cs + 512], in0=xres[:, tb, 512 * cs:512 * cs + 512], scalar=ALPHA, in1=pO[:],
                            op0=ALU.mult, op1=ALU.add), reads=[bpO, b_xres[tb]], writes=[b_xres[tb]])
                for tb in range(4):
                    layer_norm(tb, 1)
                    r0 = tok0 + 128 * tb
                    S.dma("pool", out_d[r0:r0 + 128, :], xres[:, tb, :], reads=[b_xres[tb]], is_output=True)

        except _Stop:
            pass
        with nc.Block() as block:
            S.emit(block)
    return nc


def _consts():
    c = np.zeros((128, NCST), np.float32)
    c[:, C_ID:C_ID + 128] = np.eye(128)
    i = np.arange(128)
    c[:, C_U:C_U + 128] = (i[:, None] <= i[None, :])
    c[:, C_ONES:C_ONES + 128] = 1.0
    c[:, C_NEGM:C_NEGM + 128] = np.where(i[:, None] <= i[None, :], 0.0, NEG)
    c[:, C_NEGMP:C_NEGMP + 128] = np.where(i[:, None] > i[None, :], 0.0, NEG)
    pm = np.zeros((128, 128), np.float32)
    for f2 in range(128):
        d = f2 % 64
        if d < 8:
            pm[f2 + 8, f2] = 1.0
        elif d < 16:
            pm[f2 - 8, f2] = 1.0
    c[:, C_PMAT:C_PMAT + 128] = pm
    c[:, C_ONE_E:C_ONE_E + 64] = 1.0
    c[:, C_ONE_O + 64:C_ONE_O + 128] = 1.0
    inv = (500000.0 ** (-np.arange(0, 16, 2, dtype=np.float32) / 16)).astype(np.float32)
    for p in range(128):
        d = p % 64
        if d < 16:
            c[p, C_FREQ] = inv[d % 8]
            c[p, C_SGN] = -1.0 if d < 8 else 1.0
    return c


def _params(a_conv_w, a_conv_b, a_dt_bias, a_log, a_d, a_norm_w, kv_b, b_q_bias, b_sinks):
    p = np.zeros((128, NPAR), np.float32)
    cw = a_conv_w[0].reshape(4, 48, 128)
    p[:, P_CW:P_CW + 192] = cw.transpose(2, 1, 0).reshape(128, 192)
    p[:, P_CB:P_CB + 48] = a_conv_b[0].reshape(48, 128).T
    p[:, P_DTB:P_DTB + 64] = a_dt_bias[0][None, :]
    p[:, P_ALOG:P_ALOG + 64] = a_log[0][None, :]
    p[:, P_AD:P_AD + 64] = a_d[0][None, :]
    p[:, P_NW:P_NW + 32] = a_norm_w[0].reshape(32, 128).T
    kb = kv_b[:256].reshape(4, 64)
    p[:, P_KB:P_KB + 4] = np.concatenate([kb, kb], axis=1).T
    p[:, P_VB:P_VB + 256] = kv_b[256:][None, :]
    p[:, P_QB:P_QB + 16] = b_q_bias[0].reshape(16, 128).T
    sk = b_sinks[0].reshape(16, 2)
    p[:64, P_SINK:P_SINK + 16] = sk[:, 0][None, :]
    p[64:, P_SINK:P_SINK + 16] = sk[:, 1][None, :]
    return p


def make_in_maps(inputs, cores):
    f = lambda a: np.ascontiguousarray(np.asarray(a, dtype=np.float32))
    cst = _consts()
    par = _params(*[np.asarray(inputs[k], np.float32) for k in
                    ["a_conv_w", "a_conv_b", "a_dt_bias", "a_log", "a_d", "a_norm_w", "kv_b", "b_q_bias", "b_sinks"]])
    g = np.asarray(inputs["ln_g"], np.float32)
    b = np.asarray(inputs["ln_b"], np.float32)
    lng = np.stack([np.broadcast_to(v[None, :], (128, D)) for v in (g[0], b[0], g[1], b[1])]).astype(np.float32)
    shared = {"w_in": f(inputs["a_w_in"][0]), "w_out": f(inputs["a_w_out"][0]), "kv_w": f(inputs["kv_w"]),
              "bw_in": f(inputs["b_w_in"][0]), "bw_out": f(inputs["b_w_out"][0]), "par": par, "cst": cst,
              "lng": np.ascontiguousarray(lng)}
    maps = []
    for c in cores:
        m = dict(shared)
        m["x"] = f(inputs["x"][c])
        m["pos"] = np.ascontiguousarray(np.asarray(inputs["positions"][c], np.int32).reshape(1, SEQ))
        maps.append(m)
    return maps


def kernel(**inputs):
    nc = build(NT=SEQ // T)
    maps = make_in_maps(inputs, list(range(8)))
    res = run_bass_kernel_spmd(nc, maps, core_ids=list(range(8)))
    return np.stack([r["out"] for r in res.results], axis=0).astype(np.float32)
```

```python
import math
from contextlib import ExitStack
import numpy as np
import concourse.bass as bass
import concourse.mybir as mybir
from concourse.ap import AP
from concourse.bass_utils import run_bass_kernel_spmd

F32 = mybir.dt.float32
BF = mybir.dt.bfloat16
I32 = mybir.dt.int32
AF = mybir.ActivationFunctionType
ALU = mybir.AluOpType

D = 2048
SEQ = 4096
T = 512
DIN = 4096
NPROJ = 10304
ALPHA = (2.0 * 2) ** 0.25
EPS = 1e-5
NEG = -30000.0

C_ID, C_U, C_ONES, C_NEGM, C_NEGMP, C_PMAT, C_ONE_E, C_ONE_O, C_FREQ, C_SGN = 0, 128, 256, 384, 512, 640, 768, 896, 1024, 1025
NCST = 1028
P_CW, P_CB, P_DTB, P_ALOG, P_AD, P_NW, P_KB, P_VB, P_QB, P_SINK = 0, 192, 240, 304, 368, 432, 464, 468, 724, 740
NPAR = 756

SL_Z, SL_XS, SL_B, SL_C, SL_WO, SL_K, SL_Q, SL_G, SL_O2 = 0, 8, 16, 18, 20, 28, 29, 33, 37
NSLAB = 41


class Buf:
    __slots__ = ("name", "w", "r", "excl")

    def __init__(self, name="", excl=False):
        self.name = name
        self.w = None
        self.r = []
        self.excl = excl


class _Rec:
    def __getattr__(self, name):
        def f(*a, **k):
            self.call = (name, a, k)
            return self
        return f


class Sched:
    CH = 2000
    ENGS = ("pe", "act", "dve", "pool", "sp")
    SAME_SYNC = {"pe": False, "act": True, "dve": True, "pool": True, "sp": False}

    def __init__(self, nc, es, ring=16):
        self.nc = nc
        self.es = es
        self.q = {e: [] for e in self.ENGS}
        self.cnt = {e: 0 for e in self.ENGS}
        self.sem = {e: None for e in self.ENGS}
        self.waited = {e: {} for e in self.ENGS}
        self.nsem = 0
        self.ring = {e: [] for e in ("sp", "pool", "act")}
        self.ringn = {e: 0 for e in ("sp", "pool", "act")}
        self.ringsz = ring
        self.out_tokens = []

    def _newsem(self, name):
        self.nsem += 1
        return self.es.enter_context(self.nc.semaphore(f"{name}{self.nsem}"))

    def _need(self, eng, tok, waits):
        if tok is None:
            return
        src, sem, val = tok
        if src == eng and not self.SAME_SYNC[eng]:
            return
        key = id(sem)
        if self.waited[eng].get(key, 0) >= val:
            return
        self.waited[eng][key] = val
        waits.append((sem, val))

    def _deps(self, eng, reads, writes):
        waits = []
        for b in reads:
            self._need(eng, b.w, waits)
            if b.excl:
                for t in b.r:
                    if t[0] != eng:
                        self._need(eng, t, waits)
        for b in writes:
            self._need(eng, b.w, waits)
            for t in b.r:
                self._need(eng, t, waits)
        return waits

    def op(self, eng, fn, reads=(), writes=(), signal=True):
        rec = _Rec()
        fn(rec)
        fn = rec.call
        if self.sem[eng] is None or self.cnt[eng] >= self.CH:
            self.sem[eng] = self._newsem("e_" + eng)
            self.cnt[eng] = 0
        tok = (eng, self.sem[eng], self.cnt[eng] + 1)
        waits = self._deps(eng, reads, writes)
        if signal:
            self.cnt[eng] += 1
        self.q[eng].append((waits, fn, self.sem[eng] if signal else None, 1))
        for b in reads:
            b.r.append(tok)
        for b in writes:
            b.w = tok
            b.r = []
        return tok

    def dma(self, eng, out, in_, reads=(), writes=(), is_output=False, **kw):
        r = self.ring[eng]
        i = self.ringn[eng] % self.ringsz
        self.ringn[eng] += 1
        if i >= len(r):
            r.append([self._newsem("d_" + eng), 0])
        sem, k = r[i]
        waits = self._deps(eng, reads, writes)
        if k > 0:
            self._need(eng, (None, sem, 16 * k), waits)
        r[i][1] = k + 1
        tok = (None, sem, 16 * (k + 1))
        self.q[eng].append((waits, lambda e: e.dma_start(out=out, in_=in_, **kw), sem, 16))
        for b in reads:
            b.r.append(tok)
        for b in writes:
            b.w = tok
            b.r = []
        if is_output:
            self.out_tokens.append(tok)
        return tok

    def fence(self, engs=("pe", "act", "dve", "pool")):
        toks = []
        for e in engs:
            if self.sem[e] is None or self.cnt[e] == 0:
                continue
            toks.append((e, self.sem[e], self.cnt[e]))
        for e in engs:
            waits = []
            for t in toks:
                if t[0] != e:
                    self._need(e, t, waits)
            for sem, k in self.ring["pool"]:
                if k > 0:
                    self._need(e, (None, sem, 16 * k), waits)
            if waits:
                self.q[e].append((waits, None, None, 0))

    def emit(self, block):
        nc = self.nc
        fin = []
        for t in self.out_tokens:
            self._need("pool", t, fin)
        if fin:
            self.q["pool"].append((fin, None, None, 0))

        def run(eng_name):
            def body(e):
                for waits, fn, sem, inc in self.q[eng_name]:
                    for (s, v) in waits:
                        e.wait_ge(s, v)
                    if fn is not None:
                        if callable(fn):
                            ins = fn(e)
                        else:
                            ins = getattr(e, fn[0])(*fn[1], **fn[2])
                        if sem is not None:
                            ins.then_inc(sem, inc)
            return body
        block.tensor(run("pe"))
        block.scalar(run("act"))
        block.vector(run("dve"))
        block.gpsimd(run("pool"))
        block.sync(run("sp"))


def bc(ap, n):
    return AP(ap.tensor, ap.offset, [list(x) for x in ap.ap] + [[0, n]])


class _Stop(Exception):
    pass


def build(NT=8, dbg=False, stop_after=None):
    def stage(name):
        if stop_after == name:
            raise _Stop()
    nc = bass.Bass("TRN2", target_bir_lowering=False)
    x_d = nc.dram_tensor("x", [SEQ, D], F32, kind="ExternalInput").ap()
    pos_d = nc.dram_tensor("pos", [1, SEQ], I32, kind="ExternalInput").ap()
    win_d = nc.dram_tensor("w_in", [D, NPROJ], F32, kind="ExternalInput").ap()
    wout_d = nc.dram_tensor("w_out", [DIN, D], F32, kind="ExternalInput").ap()
    kvw_d = nc.dram_tensor("kv_w", [D, 512], F32, kind="ExternalInput").ap()
    bwin_d = nc.dram_tensor("bw_in", [D, 4096], F32, kind="ExternalInput").ap()
    bwout_d = nc.dram_tensor("bw_out", [D, D], F32, kind="ExternalInput").ap()
    par_d = nc.dram_tensor("par", [128, NPAR], F32, kind="ExternalInput").ap()
    cst_d = nc.dram_tensor("cst", [128, NCST], F32, kind="ExternalInput").ap()
    lng_d = nc.dram_tensor("lng", [4, 128, D], F32, kind="ExternalInput").ap()
    out_d = nc.dram_tensor("out", [SEQ, D], F32, kind="ExternalOutput").ap()
    if dbg:
        dbg_d = nc.dram_tensor("dbg", [NT * T, D], F32, kind="ExternalOutput").ap()
    wb_d = nc.dram_tensor("wb", [NSLAB, 128, 16, 512], BF).ap()
    wbdt_d = nc.dram_tensor("wbdt", [128, 16, 64], BF).ap()
    wbv_d = nc.dram_tensor("wbv", [128, 16, 256], BF).ap()

    es = ExitStack()
    with es:
        def sb(name, shape, dt):
            return es.enter_context(nc.sbuf_tensor(name, shape, dt))
        S = Sched(nc, es)
        cst = sb("cst_sb", [128, NCST], F32)
        par = sb("par_sb", [128, NPAR], F32)
        identb = sb("identb", [128, 128], BF)
        negm4 = sb("negm4", [128, 512], BF)
        negmp4 = sb("negmp4", [128, 512], BF)
        pmatb = sb("pmatb", [128, 128], BF)
        oneE = sb("oneE", [128, 128], BF)
        oneO = sb("oneO", [128, 128], BF)
        Abc = sb("Abc", [128, 64], F32)
        expsink = sb("expsink", [128, 16], F32)
        lng = sb("lng_sb", [128, 2, D], F32)
        NSLOT = 2
        wslot = [sb(f"wslot{i}", [128, 16, 512], BF) for i in range(NSLOT)]
        wdt = sb("wdt", [128, 16, 64], BF)
        xT = sb("xT", [128, 16, T], BF)
        xin = sb("xin", [128, D], F32)
        Sst = sb("Sst", [128, 8, 512], F32)
        Sbf4 = [sb(f"Sbf{i}", [128, 512], BF) for i in range(4)]
        halo = sb("halo", [128, 48, 3], F32)
        Kcar = sb("Kcar", [128, 4, 128], BF)
        Vcar = sb("Vcar", [128, 4, 2, 128], BF)
        small = sb("small", [128, 16], F32)
        bnst = sb("bnst", [128, 4, 6], F32)
        ARENA = 96 * 1024
        arena = sb("arena", [128, ARENA // 2], BF)

        def carve(off, shape, dt):
            n = int(np.prod(shape[1:]))
            if dt == F32:
                assert off % 4 == 0
                v = arena[:, off // 2: off // 2 + 2 * n].bitcast(F32)
            else:
                v = arena[:, off // 2: off // 2 + n]
            if len(shape) == 3:
                v = v.rearrange("p (a b) -> p a b", a=shape[1])
            elif len(shape) == 4:
                v = v.rearrange("p (a b c) -> p a b c", a=shape[1], b=shape[2])
            return v
        K = 1024
        yT = carve(0, [128, 32, T], BF)
        oT = carve(0, [128, 16, T], BF)
        qT = carve(16 * K, [128, 4, T], BF)
        gT = carve(20 * K, [128, 4, T], BF)
        PT = [[carve(24 * K + (e * 2 + kb) * K, [128, 512], BF) for kb in range(2)] for e in range(2)]
        PT2 = [[carve(28 * K + (e * 2 + kb) * K, [128, 512], BF) for kb in range(2)] for e in range(2)]
        xres = carve(32 * K, [128, 4, D], F32)
        o = 32 * K
        xdt = carve(o, [128, 4, 512], BF); o += 4 * K
        xsD = carve(o, [128, 4, 512], BF); o += 4 * K
        siluz = carve(o, [128, 4, 512], BF); o += 4 * K
        MT = [carve(o + h * K, [128, 512], BF) for h in range(8)]; o += 8 * K
        LT = [carve(o + i * K, [128, 512], BF) for i in range(2)]; o += 2 * K
        CBT = carve(o, [128, 512], BF); o += K
        Btok = carve(o, [128, 4, 128], BF); o += K
        xdtw = carve(o, [128, 512], BF); o += K
        yn2 = [carve(o, [128, 512], BF), carve(o + K, [128, 512], BF)]; o += 2 * K
        ybuf = [carve(o, [128, 512], F32), carve(o + 2 * K, [128, 512], F32)]; o += 4 * K
        assert o <= 64 * K, o
        o = 64 * K
        BCT = carve(o, [128, 8, T], BF); o += 8 * K
        xsT2 = [carve(o, [128, 4, T], BF), carve(o + 4 * K, [128, 4, T], BF)]; o += 8 * K
        xpre2 = [carve(o, [128, 516], F32), carve(o + 2 * K + 16, [128, 516], F32)]; o += 4 * K + 32
        acc2 = [carve(o, [128, 512], F32), carve(o + 2 * K, [128, 512], F32)]; o += 4 * K
        dt_tok = carve(o, [128, 4, 64], F32); o += K
        negcum = carve(o, [128, 4, 64], F32); o += K
        expcum = carve(o, [128, 4, 64], F32); o += K
        dst = carve(o, [128, 4, 64], F32); o += K
        etot = carve(o, [128, 4, 64], F32); o += K
        cumT = carve(o, [128, 512], F32); o += 2 * K
        da = carve(o, [128, 64], F32); o += 256
        dtmp = carve(o, [128, 64], F32); o += 256
        assert o <= 96 * K, o
        o = 64 * K
        KT = carve(o, [128, 4, 640], BF); o += 5 * K
        Vp = carve(o, [128, 5, 4, 256], BF); o += 10 * K
        cosF = carve(o, [128, 512], F32); o += 2 * K
        sinF = carve(o, [128, 512], F32); o += 2 * K
        posi = carve(o, [128, 512], I32 if False else F32).bitcast(I32); o += 2 * K
        ang = carve(o, [128, 512], F32); o += 2 * K
        rtmp = carve(o, [128, 512], F32); o += 2 * K
        ktmp = posi
        qraw = carve(o, [128, 512], BF); o += K
        qa = carve(o, [128, 512], F32); o += 2 * K
        den = carve(o, [128, 512], F32); o += 2 * K
        attn = carve(o, [128, 512], F32); o += 2 * K
        assert o <= 96 * K, o
        xstage = sb("xstage", [128, 512], F32)

        ps = [es.enter_context(nc.psum_tensor(f"ps{i}", [128, 512], F32)) for i in range(8)]
        pb = [Buf(f"ps{i}", excl=True) for i in range(8)]

        b_cst, b_par, b_const2 = Buf(), Buf(), Buf()
        b_slab = [Buf(f"slab{i}") for i in range(NSLAB)]
        b_wbdt, b_wbv = Buf(), Buf()
        b_slot = [Buf(f"slot{i}") for i in range(NSLOT)]
        b_wdt, b_wv = Buf(), Buf()
        b_xT, b_xin, b_xstage = Buf("xT"), Buf("xin"), Buf("xstage")
        b_S = [Buf(f"S{g}") for g in range(8)]
        b_Sbf4 = [Buf() for _ in range(4)]
        b_halo = [Buf() for _ in range(48)]
        b_lng = [Buf(), Buf()]
        b_small = Buf()
        b_car = Buf()
        b_x1d = Buf()
        slot_rr = [0]

        S.dma("pool", cst[:], cst_d, writes=[b_cst])
        S.dma("pool", par[:], par_d, writes=[b_par])
        cb = [b_cst, b_par]
        S.op("dve", lambda e: e.tensor_copy(out=identb[:], in_=cst[:, C_ID:C_ID + 128]), reads=cb, writes=[b_const2])
        for j in range(4):
            S.op("dve", lambda e, j=j: e.tensor_copy(out=negm4[:, 128 * j:128 * j + 128], in_=cst[:, C_NEGM:C_NEGM + 128]), reads=cb, writes=[b_const2])
            S.op("dve", lambda e, j=j: e.tensor_copy(out=negmp4[:, 128 * j:128 * j + 128], in_=cst[:, C_NEGMP:C_NEGMP + 128]), reads=cb, writes=[b_const2])
        S.op("dve", lambda e: e.tensor_copy(out=pmatb[:], in_=cst[:, C_PMAT:C_PMAT + 128]), reads=cb, writes=[b_const2])
        S.op("dve", lambda e: e.tensor_copy(out=oneE[:], in_=cst[:, C_ONE_E:C_ONE_E + 128]), reads=cb, writes=[b_const2])
        S.op("dve", lambda e: e.tensor_copy(out=oneO[:], in_=cst[:, C_ONE_O:C_ONE_O + 128]), reads=cb, writes=[b_const2])
        S.op("act", lambda e: e.activation(out=Abc[:], in_=par[:, P_ALOG:P_ALOG + 64], func=AF.Exp), reads=cb, writes=[b_const2])
        S.op("act", lambda e: e.mul(out=Abc[:], in_=Abc[:], mul=-1.0), reads=cb, writes=[b_const2])
        S.op("act", lambda e: e.activation(out=expsink[:], in_=par[:, P_SINK:P_SINK + 16], func=AF.Exp), reads=cb, writes=[b_const2])
        S.op("pool", lambda e: e.memset(small[:, 8:9], EPS), writes=[b_small])
        S.op("pool", lambda e: e.memset(small[:, 9:10], -0.5), writes=[b_small])
        S.op("pool", lambda e: e.memset(halo[:], 0.0), writes=b_halo)
        S.op("pool", lambda e: e.memset(Sst[:], 0.0), writes=b_S)
        S.op("pool", lambda e: e.memset(Kcar[:], 0.0), writes=[b_car])
        S.op("pool", lambda e: e.memset(Vcar[:], 0.0), writes=[b_car])
        cb = [b_cst, b_par, b_const2]

        conv_src = {}
        conv_src[SL_B] = (win_d, 0, 8192); conv_src[SL_B + 1] = (win_d, 0, 8192 + 512)
        conv_src[SL_C] = (win_d, 0, 9216); conv_src[SL_C + 1] = (win_d, 0, 9216 + 512)
        for g in range(8):
            conv_src[SL_XS + g] = (win_d, 0, 4096 + 512 * g)
            conv_src[SL_Z + g] = (win_d, 0, 512 * g)
        for cs in range(4):
            for kh in range(2):
                conv_src[SL_WO + cs * 2 + kh] = (wout_d, 2048 * kh, 512 * cs)
        for g in range(4):
            conv_src[SL_Q + g] = (bwin_d, 0, 512 * g)
            conv_src[SL_G + g] = (bwin_d, 0, 2048 + 512 * g)
        for cs in range(4):
            conv_src[SL_O2 + cs] = (bwout_d, 0, 512 * cs)
        use_order = [SL_B, SL_C, SL_XS]
        for g in range(8):
            if g == 4:
                use_order += [SL_B + 1, SL_C + 1]
            if g + 1 < 8:
                use_order.append(SL_XS + g + 1)
            use_order.append(SL_Z + g)
        use_order += [SL_WO + i for i in range(8)] + [SL_K, "V"]
        for g in range(4):
            use_order += [SL_Q + g, SL_G + g]
        use_order += [SL_O2 + i for i in range(4)]
        conv_done = set()
        conv_ptr = [0]
        LOOKAHEAD = 8

        def do_convert(idx):
            if idx in conv_done:
                return
            conv_done.add(idx)
            if idx == "V":
                S.dma("pool", wbv_d, kvw_d[:, 256:512].rearrange("(kc p) c -> p kc c", p=128), writes=[b_wbv])
            elif idx == SL_K:
                for g in range(4):
                    ksrc = kvw_d[:, 64 * g:64 * g + 64].rearrange("(kc p) d -> p kc d", p=128)
                    for e2 in range(2):
                        S.dma("pool", wb_d[SL_K][:, :, 128 * g + 64 * e2:128 * g + 64 * e2 + 64], ksrc, writes=[b_slab[SL_K]])
            else:
                src, r0, c0 = conv_src[idx]
                v = src[r0:r0 + 2048, c0:c0 + 512].rearrange("(kc p) c -> p kc c", p=128)
                S.dma("pool", wb_d[idx], v, writes=[b_slab[idx]])

        def ensure_converted(idx):
            if idx not in conv_done:
                while conv_ptr[0] < len(use_order):
                    j = use_order[conv_ptr[0]]
                    conv_ptr[0] += 1
                    do_convert(j)
                    if j == idx:
                        break
                do_convert(idx)
            k = 0
            while conv_ptr[0] < len(use_order) and k < LOOKAHEAD:
                do_convert(use_order[conv_ptr[0]])
                conv_ptr[0] += 1
                k += 1

        S.dma("pool", wbdt_d, win_d[:, 10240:10304].rearrange("(kc p) c -> p kc c", p=128), writes=[b_wbdt])
        S.dma("sp", wdt[:], wbdt_d, reads=[b_wbdt], writes=[b_wdt])

        def load_slab(idx):
            ensure_converted(idx)
            i = slot_rr[0] % NSLOT
            slot_rr[0] += 1
            S.dma("sp", wslot[i][:], wb_d[idx], reads=[b_slab[idx]], writes=[b_slot[i]])
            return wslot[i], b_slot[i]

        evac_rr = [0]

        def mm_group(pbank, pbuf, mms, reads):
            n = len(mms)
            for i, (o_, l_, r_, st, sp_) in enumerate(mms):
                S.op("pe", lambda e, o_=o_, l_=l_, r_=r_, st=st, sp_=sp_: e.matmul(o_, lhsT=l_, rhs=r_, start=st, stop=sp_),
                     reads=reads, writes=[pbuf], signal=(i == n - 1))

        pj_rr = [0]

        pj_pending = {}

        def next_pj():
            i = pj_rr[0] % 2
            pj_rr[0] += 1
            if i in pj_pending:
                if (1 - i) not in pj_pending:
                    i = 1 - i
                    pj_rr[0] += 1
                else:
                    pj_pending.pop(i)()
            return ps[i], pb[i]

        try:
            for t in range(NT):
                tok0 = t * T
                bA = {n: Buf(n) for n in ["BCT", "xsT", "xdt", "xsD", "xstok", "siluz", "CBT", "Btok", "xdtw", "yn", "sqj",
                                          "toff", "yv", "y2", "xpre", "acc", "dtv", "cumT", "da", "dtmp", "yT", "LT0", "LT1"]}
                bMT = [Buf(f"MT{h}") for h in range(8)]
                bxsT = [Buf("xsT0"), Buf("xsT1")]
                bybuf = [Buf("yb0"), Buf("yb1")]
                byn2 = [Buf("yn0"), Buf("yn1")]
                bsmall2 = [Buf("sm0"), Buf("sm1")]
                bxp2 = [Buf("xp0"), Buf("xp1")]
                bac2 = [Buf("ac0"), Buf("ac1")]
                conv_rr = [0]
                S.dma("pool", lng[:, 0, :], lng_d[0], writes=[b_lng[0]])
                S.dma("pool", lng[:, 1, :], lng_d[1], writes=[b_lng[1]])
                pre_BC = [load_slab(SL_B), load_slab(SL_C)]
                for tb in range(4):
                    S.dma("sp", xin[:], x_d[tok0 + 128 * tb: tok0 + 128 * tb + 128, :], writes=[b_xin])
                    for q4 in range(4):
                        pbank, pbuf = ps[2 + (q4 % 2)], pb[2 + (q4 % 2)]
                        mm_group(pbank, pbuf, [(pbank[:, 128 * j:128 * j + 128], xin[:, (4 * q4 + j) * 128:(4 * q4 + j) * 128 + 128],
                                                cst[:, C_ID:C_ID + 128], True, True) for j in range(4)], reads=[b_xin] + cb)
                        eng = "act" if (q4 % 2) else "dve"
                        outap = xT[:, 4 * q4:4 * q4 + 4, 128 * tb:128 * tb + 128]
                        inap = pbank[:].rearrange("p (a b) -> p a b", a=4)
                        if eng == "act":
                            S.op("act", lambda e, o_=outap, i_=inap: e.activation(out=o_, in_=i_, func=AF.Copy), reads=[pbuf], writes=[b_xT])
                        else:
                            S.op("dve", lambda e, o_=outap, i_=inap: e.tensor_copy(out=o_, in_=i_), reads=[pbuf], writes=[b_xT])
                stage('A1')
                for tb in range(4):
                    pbank, pbuf = ps[3], pb[3]
                    mm_group(pbank, pbuf, [(pbank[:, 0:64], xT[:, kc, 128 * tb:128 * tb + 128], wdt[:, kc, :], kc == 0, kc == 15)
                                           for kc in range(16)], reads=[b_xT, b_wdt])
                    S.op("dve", lambda e: e.tensor_tensor(out=dtmp[:], in0=pbank[:, 0:64], in1=par[:, P_DTB:P_DTB + 64], op=ALU.add),
                         reads=[pbuf] + cb, writes=[bA["dtmp"]])
                    S.op("act", lambda e: e.activation(out=dtmp[:], in_=dtmp[:], func=AF.Exp), reads=[bA["dtmp"]], writes=[bA["dtmp"]])
                    S.op("act", lambda e, tb=tb: e.activation(out=dt_tok[:, tb, :], in_=dtmp[:], func=AF.Ln, bias=1.0),
                         reads=[bA["dtmp"]], writes=[bA["dtv"]])
                    S.op("dve", lambda e, tb=tb: e.tensor_tensor(out=da[:], in0=dt_tok[:, tb, :], in1=Abc[:], op=ALU.mult),
                         reads=[bA["dtv"]] + cb, writes=[bA["da"]])
                    mm_group(pbank, pbuf, [(pbank[:, 0:64], cst[:, C_U:C_U + 128], da[:], True, True),
                                           (pbank[:, 64:128], cst[:, C_ONES:C_ONES + 128], da[:], True, True),
                                           (pbank[0:64, 128:256], da[:], cst[:, C_U:C_U + 128], True, True)],
                             reads=[bA["da"]] + cb)
                    S.op("act", lambda e, tb=tb: e.mul(out=negcum[:, tb, :], in_=pbank[:, 0:64], mul=-1.0), reads=[pbuf], writes=[bA["dtv"]])
                    S.op("act", lambda e, tb=tb: e.activation(out=expcum[:, tb, :], in_=pbank[:, 0:64], func=AF.Exp), reads=[pbuf], writes=[bA["dtv"]])
                    S.op("act", lambda e, tb=tb: e.activation(out=etot[:, tb, :], in_=pbank[:, 64:128], func=AF.Exp), reads=[pbuf], writes=[bA["dtv"]])
                    S.op("dve", lambda e, tb=tb: e.tensor_tensor(out=dtmp[:], in0=pbank[:, 64:128], in1=negcum[:, tb, :], op=ALU.add),
                         reads=[pbuf, bA["dtv"]], writes=[bA["dtmp"]])
                    S.op("act", lambda e, tb=tb: e.activation(out=dst[:, tb, :], in_=dtmp[:], func=AF.Exp), reads=[bA["dtmp"]], writes=[bA["dtv"]])
                    S.op("dve", lambda e, tb=tb: e.tensor_copy(out=cumT[0:64, 128 * tb:128 * tb + 128], in_=pbank[0:64, 128:256]),
                         reads=[pbuf], writes=[bA["cumT"]])

                stage('A2')

                def proj_fm_conv(slot, bslot, fc, ch, dest, bdest, defer=False):
                    pbank, pbuf = next_pj()
                    ci = conv_rr[0] % 2
                    conv_rr[0] += 1
                    xpre, acc = xpre2[ci], acc2[ci]
                    bxpre, bacc = bxp2[ci], bac2[ci]
                    mm_group(pbank, pbuf, [(pbank[:], slot[:, kc, 128 * fc:128 * fc + 128], xT[:, kc, :], kc == 0, kc == 15)
                                           for kc in range(16)], reads=[bslot, b_xT])

                    bank_i = 0 if pbank is ps[0] else 1
                    done = [False]

                    def evac():
                        if done[0]:
                            return
                        done[0] = True
                        pj_pending.pop(bank_i, None)
                        proj_fm_conv_ev(pbank, pbuf, xpre, acc, bxpre, bacc, ch, dest, bdest)
                    if defer:
                        pj_pending[bank_i] = evac
                        return evac
                    evac()

                def proj_fm_conv_ev(pbank, pbuf, xpre, acc, bxpre, bacc, ch, dest, bdest):
                    S.op("pool", lambda e: e.tensor_copy(out=xpre[:, 0:3], in_=halo[:, ch, :]), reads=[b_halo[ch]], writes=[bxpre])
                    S.op("act", lambda e: e.activation(out=xpre[:, 3:515], in_=pbank[:], func=AF.Copy), reads=[pbuf], writes=[bxpre])
                    S.op("pool", lambda e: e.tensor_copy(out=halo[:, ch, :], in_=xpre[:, 512:515]), reads=[bxpre], writes=[b_halo[ch]])
                    S.op("act", lambda e: e.activation(out=acc[:], in_=pbank[:], func=AF.Identity,
                                                       bias=par[:, P_CB + ch:P_CB + ch + 1], scale=par[:, P_CW + 4 * ch + 3:P_CW + 4 * ch + 4]),
                         reads=[pbuf] + cb, writes=[bacc])
                    for j in (2, 1, 0):
                        S.op("dve", lambda e, j=j: e.scalar_tensor_tensor(out=acc[:], in0=xpre[:, j:j + 512],
                                                                          scalar=par[:, P_CW + 4 * ch + j:P_CW + 4 * ch + j + 1],
                                                                          in1=acc[:], op0=ALU.mult, op1=ALU.add),
                             reads=[bxpre, bacc] + cb, writes=[bacc])
                    S.op("act", lambda e: e.activation(out=dest, in_=acc[:], func=AF.Silu), reads=[bacc], writes=[bdest])

                xs_state = {}

                def xs_load(g):
                    xs_state[g] = load_slab(SL_XS + g)

                def xs_proj_chunk(g, fc, defer=False):
                    slot, bslot = xs_state[g]
                    return proj_fm_conv(slot, bslot, fc, 4 * g + fc, xsT2[g % 2][:, fc, :], bxsT[g % 2], defer=defer)

                for half in range(2):
                    slot, bslot = pre_BC[0] if half == 0 else load_slab(SL_B + half)
                    for fc in range(4):
                        proj_fm_conv(slot, bslot, fc, 32 + 4 * half + fc, BCT[:, fc, :], bA["BCT"])
                    slot, bslot = pre_BC[1] if half == 0 else load_slab(SL_C + half)
                    for fc in range(4):
                        proj_fm_conv(slot, bslot, fc, 40 + 4 * half + fc, BCT[:, 4 + fc, :], bA["BCT"])
                    if half == 0:
                        xs_load(0)
                        for fc in range(4):
                            xs_proj_chunk(0, fc)
                        if t > 0:
                            S.fence()
                    for gl in range(4):
                        g = 4 * half + gl
                        BTg = BCT[:, gl, :]
                        CTg = BCT[:, 4 + gl, :]
                        xsT = xsT2[g % 2]
                        bxs = bxsT[g % 2]
                        if g + 1 < 8:
                            xs_load(g + 1)
                        S.op("act", lambda e, g=g: e.activation(out=Sbf4[0][:], in_=Sst[:, g, :], func=AF.Copy), reads=[b_S[g]], writes=[b_Sbf4[0]])
                        for c in range(4):
                            pbank, pbuf = ps[2 + c % 2], pb[2 + c % 2]
                            mm_group(pbank, pbuf, [(pbank[:, 128 * fc:128 * fc + 128], xsT[:, fc, 128 * c:128 * c + 128], identb[:], True, True)
                                                   for fc in range(4)], reads=[bxs] + cb)
                            p3 = pbank[:].rearrange("p (h d) -> p h d", d=64)
                            S.op("dve", lambda e, c=c, g=g, p3=p3: e.tensor_tensor(out=xdt[:, c, :].rearrange("p (h d) -> p h d", d=64), in0=p3,
                                                                                   in1=bc(dt_tok[:, c, 8 * g:8 * g + 8], 64), op=ALU.mult),
                                 reads=[pbuf, bA["dtv"]], writes=[bA["xdt"]])
                            S.op("dve", lambda e, c=c, g=g, p3=p3: e.tensor_tensor(out=xsD[:, c, :].rearrange("p (h d) -> p h d", d=64), in0=p3,
                                                                                   in1=bc(par[:, P_AD + 8 * g:P_AD + 8 * g + 8], 64), op=ALU.mult),
                                 reads=[pbuf] + cb, writes=[bA["xsD"]])
                        pbank, pbuf = ps[2], pb[2]
                        mm_group(pbank, pbuf, [(pbank[:, 128 * c:128 * c + 128], BTg[:, 128 * c:128 * c + 128], identb[:], True, True)
                                               for c in range(4)], reads=[bA["BCT"]] + cb)
                        S.op("act", lambda e, pbank=pbank: e.activation(out=Btok[:].rearrange("p a b -> p (a b)"), in_=pbank[:], func=AF.Copy),
                             reads=[pbuf], writes=[bA["Btok"]])
                        pbank, pbuf = ps[3], pb[3]
                        mm_group(pbank, pbuf, [(pbank[:, 128 * c:128 * c + 128], BTg[:, 128 * c:128 * c + 128], CTg[:, 128 * c:128 * c + 128], True, True)
                                               for c in range(4)], reads=[bA["BCT"]])
                        S.op("act", lambda e, pbank=pbank: e.activation(out=CBT[:], in_=pbank[:], func=AF.Copy), reads=[pbuf], writes=[bA["CBT"]])
                        zslot = load_slab(SL_Z + g)

                        def FILLZ(tb, zslot=zslot):
                            slot, bslot = zslot
                            pbank, pbuf = next_pj()
                            mm_group(pbank, pbuf, [(pbank[:], xT[:, kc, 128 * tb:128 * tb + 128], slot[:, kc, :], kc == 0, kc == 15)
                                                   for kc in range(16)], reads=[bslot, b_xT])
                            S.op("act", lambda e: e.activation(out=siluz[:, tb, :], in_=pbank[:], func=AF.Silu),
                                 reads=[pbuf], writes=[bA["siluz"]])

                        def state_step(c, g=g):
                            S.op("dve", lambda e: e.tensor_tensor(out=xdtw[:].rearrange("p (h d) -> p h d", d=64),
                                                                            in0=xdt[:, c, :].rearrange("p (h d) -> p h d", d=64),
                                                                            in1=bc(dst[:, c, 8 * g:8 * g + 8], 64), op=ALU.mult),
                                 reads=[bA["xdt"], bA["dtv"]], writes=[bA["xdtw"]])
                            pS, bpS = ps[6 + c % 2], pb[6 + c % 2]
                            mm_group(pS, bpS, [(pS[:], Btok[:, c, :], xdtw[:], True, True)], reads=[bA["Btok"], bA["xdtw"]])
                            S.op("dve", lambda e: e.tensor_tensor(out=Sst[:, g, :].rearrange("p (h d) -> p h d", d=64),
                                                                            in0=Sst[:, g, :].rearrange("p (h d) -> p h d", d=64),
                                                                            in1=bc(etot[:, c, 8 * g:8 * g + 8], 64), op=ALU.mult),
                                 reads=[b_S[g], bA["dtv"]], writes=[b_S[g]])
                            S.op("dve", lambda e: e.tensor_tensor(out=Sst[:, g, :], in0=pS[:], in1=Sst[:, g, :], op=ALU.add),
                                 reads=[bpS, b_S[g]], writes=[b_S[g]])
                            if c < 3:
                                S.op("act", lambda e: e.activation(out=Sbf4[c + 1][:], in_=Sst[:, g, :], func=AF.Copy),
                                     reads=[b_S[g]], writes=[b_Sbf4[c + 1]])
                        state_step(0); FILLZ(0); state_step(1); state_step(2); FILLZ(1); state_step(3)
                        for h in range(8):
                            hh = 8 * g + h
                            pbank, pbuf = ps[4 + (h % 2)], pb[4 + (h % 2)]
                            sel = cst[0:64, C_ID + hh:C_ID + hh + 1].broadcast_to([64, 128])
                            mm_group(pbank, pbuf, [(pbank[:], sel, cumT[0:64, :], True, False),
                                                   (pbank[:], identb[:], negm4[:], False, True)], reads=[bA["cumT"]] + cb)
                            lt, blt = LT[h % 2], bA[f"LT{h % 2}"]
                            for c in range(4):
                                S.op("act", lambda e, c=c, hh=hh, lt=lt, pbank=pbank: e.activation(
                                    out=lt[:, 128 * c:128 * c + 128], in_=pbank[:, 128 * c:128 * c + 128], func=AF.Exp,
                                    bias=negcum[:, c, hh:hh + 1], scale=1.0), reads=[pbuf, bA["dtv"]], writes=[blt])
                            S.op("dve", lambda e, h=h, lt=lt: e.tensor_tensor(out=MT[h][:], in0=lt[:], in1=CBT[:], op=ALU.mult),
                                 reads=[blt, bA["CBT"]], writes=[bMT[h]])
                        def Y1(c, g=g, CTg=CTg):
                            cs_ = slice(128 * c, 128 * c + 128)
                            yb, byb = ybuf[c % 2], bybuf[c % 2]
                            pA, bpA = (ps[6], pb[6]) if c % 2 == 0 else (ps[3], pb[3])
                            pB, bpB = (ps[7], pb[7]) if c % 2 == 0 else (ps[4], pb[4])
                            mms = [(pA[:], identb[:], xsD[:, c, :], True, False)]
                            for h in range(8):
                                mms.append((pA[:, 64 * h:64 * h + 64], MT[h][:, cs_], xdt[:, c, 64 * h:64 * h + 64], False, h == 7))
                            mm_group(pA, bpA, mms, reads=[bA["xsD"], bA["xdt"]] + bMT + cb)
                            mm_group(pB, bpB, [(pB[:], CTg[:, cs_], Sbf4[c][:], True, True)], reads=[bA["BCT"], b_Sbf4[c]])
                            S.op("dve", lambda e: e.tensor_tensor(out=yb[:].rearrange("p (h d) -> p h d", d=64),
                                                                  in0=pB[:].rearrange("p (h d) -> p h d", d=64),
                                                                  in1=bc(expcum[:, c, 8 * g:8 * g + 8], 64), op=ALU.mult),
                                 reads=[bpB, bA["dtv"]], writes=[byb])
                            S.op("dve", lambda e: e.tensor_tensor(out=yb[:], in0=pA[:], in1=yb[:], op=ALU.add),
                                 reads=[bpA, byb], writes=[byb])
                            S.op("dve", lambda e: e.tensor_tensor(out=yb[:], in0=yb[:], in1=siluz[:, c, :], op=ALU.mult),
                                 reads=[byb, bA["siluz"]], writes=[byb])

                        def S2(c):
                            yb, byb = ybuf[c % 2], bybuf[c % 2]
                            ynb, bynb = yn2[c % 2], byn2[c % 2]
                            sm, bsm = small[:, 10 + 3 * (c % 2):13 + 3 * (c % 2)], bsmall2[c % 2]
                            S.op("dve", lambda e: e.scalar_tensor_tensor(out=ynb[:], in0=yb[:], scalar=1.0, in1=yb[:], op0=ALU.mult, op1=ALU.mult,
                                                                         accum_out=sm[:, 0:1]), reads=[byb], writes=[bynb, bsm])
                            S.op("dve", lambda e: e.tensor_scalar(out=sm[:, 1:2], in0=sm[:, 0:1], scalar1=1.0 / 512.0, scalar2=EPS, op0=ALU.mult, op1=ALU.add),
                                 reads=[bsm], writes=[bsm])
                            S.op("pool", lambda e: e.tensor_tensor(out=sm[:, 2:3], in0=sm[:, 1:2], in1=small[:, 9:10], op=ALU.pow),
                                 reads=[bsm, b_small], writes=[bsm])
                            S.op("act", lambda e: e.activation(out=ynb[:], in_=yb[:], func=AF.Identity, scale=sm[:, 2:3]),
                                 reads=[byb, bsm], writes=[bynb])

                        def S3(c, g=g):
                            cs_ = slice(128 * c, 128 * c + 128)
                            ynb, bynb = yn2[c % 2], byn2[c % 2]
                            pT2, bpT2 = (ps[2], pb[2]) if c % 2 == 0 else (ps[5], pb[5])
                            mm_group(pT2, bpT2, [(pT2[:, 128 * fc:128 * fc + 128], ynb[:, 128 * fc:128 * fc + 128], identb[:], True, True)
                                                 for fc in range(4)], reads=[bynb] + cb)
                            for fc in range(4):
                                nwc = par[:, P_NW + 4 * g + fc:P_NW + 4 * g + fc + 1]
                                if fc < 0:
                                    S.op("act", lambda e, fc=fc: e.activation(out=yT[:, 4 * g + fc, cs_], in_=pT2[:, 128 * fc:128 * fc + 128],
                                                                              func=AF.Identity, scale=nwc), reads=[bpT2] + cb, writes=[bA["yT"]])
                                else:
                                    S.op("dve", lambda e, fc=fc: e.tensor_scalar(out=yT[:, 4 * g + fc, cs_], in0=pT2[:, 128 * fc:128 * fc + 128],
                                                                                 scalar1=nwc, scalar2=None, op0=ALU.mult), reads=[bpT2] + cb, writes=[bA["yT"]])

                        fill_ev = {}

                        def FILL(c, g=g):
                            if g + 1 < 8:
                                fill_ev[c] = xs_proj_chunk(g + 1, c, defer=True)

                        def FILLEV(c):
                            if c in fill_ev:
                                fill_ev.pop(c)()

                        Y1(0); S2(0)
                        Y1(1); S2(1); FILLZ(2); FILL(0); S3(0); FILLEV(0)
                        Y1(2); S2(2); FILLZ(3); FILL(1); S3(1); FILLEV(1)
                        Y1(3); S2(3); FILL(2); S3(2); FILLEV(2)
                        FILL(3); S3(3); FILLEV(3)

                stage('A')
                S.fence()
                bC = {n: Buf(n) for n in ["KT", "Vp", "rope", "qraw", "qa", "qT", "gT", "oT", "den", "attn", "tmp"]}
                bPT = [[Buf(), Buf()], [Buf(), Buf()]]
                bPT2 = [[Buf(), Buf()], [Buf(), Buf()]]
                pos_bc = AP(pos_d.tensor, tok0, [[0, 128], [1, 512]])
                S.dma("pool", posi, pos_bc, writes=[bC["rope"]])
                S.op("dve", lambda e: e.tensor_copy(out=ang[:], in_=posi), reads=[bC["rope"]], writes=[bC["rope"]])
                S.op("dve", lambda e: e.tensor_scalar(out=ang[:], in0=ang[:], scalar1=cst[:, C_FREQ:C_FREQ + 1], scalar2=None, op0=ALU.mult),
                     reads=[bC["rope"]] + cb, writes=[bC["rope"]])

                def sincos(dest, shift, scale_ap):
                    S.op("dve", lambda e: e.tensor_scalar(out=rtmp[:], in0=ang[:], scalar1=shift, scalar2=1.0 / (2 * math.pi),
                                                          op0=ALU.add, op1=ALU.mult), reads=[bC["rope"]], writes=[bC["tmp"]])
                    S.op("dve", lambda e: e.tensor_scalar(out=rtmp[:], in0=rtmp[:], scalar1=0.5, scalar2=None, op0=ALU.add),
                         reads=[bC["tmp"]], writes=[bC["tmp"]])
                    S.op("dve", lambda e: e.tensor_copy(out=ktmp, in_=rtmp[:]), reads=[bC["tmp"]], writes=[bC["tmp"]])
                    S.op("dve", lambda e: e.tensor_copy(out=rtmp[:], in_=ktmp), reads=[bC["tmp"]], writes=[bC["tmp"]])
                    S.op("dve", lambda e: e.scalar_tensor_tensor(out=rtmp[:], in0=rtmp[:], scalar=-2 * math.pi, in1=ang[:], op0=ALU.mult, op1=ALU.add),
                         reads=[bC["tmp"], bC["rope"]], writes=[bC["tmp"]])
                    S.op("dve", lambda e: e.tensor_scalar(out=rtmp[:], in0=rtmp[:], scalar1=shift, scalar2=None, op0=ALU.add),
                         reads=[bC["tmp"]], writes=[bC["tmp"]])
                    S.op("dve", lambda e: e.tensor_scalar(out=qa[:], in0=rtmp[:], scalar1=-math.pi, scalar2=2 * math.pi, op0=ALU.is_lt, op1=ALU.mult),
                         reads=[bC["tmp"]], writes=[bC["qa"]])
                    S.op("dve", lambda e: e.tensor_tensor(out=rtmp[:], in0=rtmp[:], in1=qa[:], op=ALU.add), reads=[bC["tmp"], bC["qa"]], writes=[bC["tmp"]])
                    S.op("dve", lambda e: e.tensor_scalar(out=qa[:], in0=rtmp[:], scalar1=math.pi, scalar2=-2 * math.pi, op0=ALU.is_gt, op1=ALU.mult),
                         reads=[bC["tmp"]], writes=[bC["qa"]])
                    S.op("dve", lambda e: e.tensor_tensor(out=rtmp[:], in0=rtmp[:], in1=qa[:], op=ALU.add), reads=[bC["tmp"], bC["qa"]], writes=[bC["tmp"]])
                    S.op("dve", lambda e: e.tensor_scalar(out=rtmp[:], in0=rtmp[:], scalar1=-3.1415925, scalar2=3.1415925, op0=ALU.max, op1=ALU.min),
                         reads=[bC["tmp"]], writes=[bC["tmp"]])
                    if scale_ap is None:
                        S.op("act", lambda e: e.activation(out=dest[:], in_=rtmp[:], func=AF.Sin), reads=[bC["tmp"]], writes=[bC["rope"]])
                    else:
                        S.op("act", lambda e: e.activation(out=dest[:], in_=rtmp[:], func=AF.Sin, scale=scale_ap), reads=[bC["tmp"]] + cb, writes=[bC["rope"]])
                sincos(cosF, math.pi / 2, None)
                sincos(sinF, 0.0, cst[:, C_SGN:C_SGN + 1])

                S.op("pool", lambda e: e.memset(Vp[:], 0.0), writes=[bC["Vp"]])
                S.op("pool", lambda e: e.tensor_copy(out=KT[:, :, 0:128], in_=Kcar[:]), reads=[b_car], writes=[bC["KT"]])
                S.op("pool", lambda e: e.tensor_copy(out=Vp[:, 0, :, :].rearrange("p g (v d) -> p g v d", v=2), in_=Vcar[:]), reads=[b_car], writes=[bC["Vp"]])

                b_xres = [Buf(f"xres{i}") for i in range(4)]
                for cs in range(4):
                    for kh in range(2):
                        slot, bslot = load_slab(SL_WO + cs * 2 + kh)
                        for tb in range(4):
                            pO, bpO = ps[4 + tb], pb[4 + tb]
                            mm_group(pO, bpO, [(pO[:], yT[:, 16 * kh + kc, 128 * tb:128 * tb + 128], slot[:, kc, :],
                                                (kh == 0 and kc == 0), (kh == 1 and kc == 15)) for kc in range(16)],
                                     reads=[bslot, bA["yT"]])
                    for tb in range(4):
                        pO, bpO = ps[4 + tb], pb[4 + tb]
                        r0 = tok0 + 128 * tb
                        S.dma("pool", xstage[:], x_d[r0:r0 + 128, 512 * cs:512 * cs + 512], writes=[b_xstage])
                        S.op("dve", lambda e, tb=tb, cs=cs, pO=pO: e.scalar_tensor_tensor(
                            out=xres[:, tb, 512 * cs:512 * cs + 512], in0=xstage[:], scalar=ALPHA, in1=pO[:], op0=ALU.mult, op1=ALU.add),
                            reads=[b_xstage, bpO], writes=[b_xres[tb]])

                def layer_norm(tb, li):
                    v = xres[:, tb, :]
                    for j in range(4):
                        S.op("dve", lambda e, j=j, v=v: e.bn_stats(out=bnst[:, j, :], in_=v[:, 512 * j:512 * j + 512]),
                             reads=[b_xres[tb]], writes=[b_small])
                    S.op("dve", lambda e: e.bn_aggr(out=small[:, 4:6], in_=bnst[:].rearrange("p a b -> p (a b)")), reads=[b_small], writes=[b_small])
                    S.op("dve", lambda e: e.tensor_scalar(out=small[:, 6:7], in0=small[:, 5:6], scalar1=EPS, scalar2=None, op0=ALU.add),
                         reads=[b_small], writes=[b_small])
                    S.op("pool", lambda e: e.tensor_tensor(out=small[:, 7:8], in0=small[:, 6:7], in1=small[:, 9:10], op=ALU.pow),
                         reads=[b_small], writes=[b_small])
                    S.op("dve", lambda e, v=v: e.scalar_tensor_tensor(out=v, in0=v, scalar=small[:, 4:5], in1=lng[:, 0, :],
                                                                      op0=ALU.subtract, op1=ALU.mult),
                         reads=[b_small, b_xres[tb], b_lng[0]], writes=[b_xres[tb]])
                    S.op("dve", lambda e, v=v: e.scalar_tensor_tensor(out=v, in0=v, scalar=small[:, 7:8], in1=lng[:, 1, :],
                                                                      op0=ALU.mult, op1=ALU.add),
                         reads=[b_small, b_xres[tb], b_lng[1]], writes=[b_xres[tb]])

                for tb in range(4):
                    layer_norm(tb, 0)
                    if dbg:
                        S.dma("pool", dbg_d[tok0 + 128 * tb:tok0 + 128 * tb + 128, :], xres[:, tb, :], reads=[b_xres[tb]], is_output=True)
                    for q4 in range(4):
                        pbank, pbuf = ps[2 + (q4 % 2)], pb[2 + (q4 % 2)]
                        mm_group(pbank, pbuf, [(pbank[:, 128 * j:128 * j + 128], xres[:, tb, (4 * q4 + j) * 128:(4 * q4 + j) * 128 + 128],
                                                cst[:, C_ID:C_ID + 128], True, True) for j in range(4)], reads=[b_xres[tb]] + cb)
                        outap = xT[:, 4 * q4:4 * q4 + 4, 128 * tb:128 * tb + 128]
                        inap = pbank[:].rearrange("p (a b) -> p a b", a=4)
                        if q4 % 2:
                            S.op("act", lambda e, o_=outap, i_=inap: e.activation(out=o_, in_=i_, func=AF.Copy), reads=[pbuf], writes=[b_xT])
                        else:
                            S.op("dve", lambda e, o_=outap, i_=inap: e.tensor_copy(out=o_, in_=i_), reads=[pbuf], writes=[b_xT])
                S.dma("pool", lng[:, 0, :], lng_d[2], writes=[b_lng[0]])
                S.dma("pool", lng[:, 1, :], lng_d[3], writes=[b_lng[1]])

                stage('B')
                S.fence()
                def rope_evac(pbank, pbuf, bias_ap, dest, bdest):
                    S.op("act", lambda e: e.activation(out=qraw[:], in_=pbank[:], func=AF.Identity, bias=bias_ap), reads=[pbuf] + cb, writes=[bC["qraw"]])
                    pP, bpP = ps[2], pb[2]
                    mm_group(pP, bpP, [(pP[:], pmatb[:], qraw[:], True, True)], reads=[bC["qraw"]] + cb)
                    S.op("pool", lambda e: e.tensor_tensor(out=qa[:], in0=qraw[:], in1=cosF[:], op=ALU.mult), reads=[bC["qraw"], bC["rope"]], writes=[bC["qa"]])
                    S.op("dve", lambda e: e.tensor_tensor(out=rtmp[:], in0=pP[:], in1=sinF[:], op=ALU.mult), reads=[bpP, bC["rope"]], writes=[bC["tmp"]])
                    S.op("dve", lambda e: e.tensor_tensor(out=dest, in0=rtmp[:], in1=qa[:], op=ALU.add), reads=[bC["tmp"], bC["qa"]], writes=[bdest])

                slot, bslot = load_slab(SL_K)
                for g in range(4):
                    pbank, pbuf = next_pj()
                    mm_group(pbank, pbuf, [(pbank[:], slot[:, kc, 128 * g:128 * g + 128], xT[:, kc, :], kc == 0, kc == 15) for kc in range(16)],
                             reads=[bslot, b_xT])
                    rope_evac(pbank, pbuf, par[:, P_KB + g:P_KB + g + 1], KT[:, g, 128:640], bC["KT"])
                ensure_converted("V")
                i_ = slot_rr[0] % NSLOT
                slot_rr[0] += 1
                S.dma("sp", wslot[i_][:, :, 0:256], wbv_d, reads=[b_wbv], writes=[b_slot[i_]])
                wv, b_wv = wslot[i_], b_slot[i_]
                for tb in range(4):
                    pbank, pbuf = next_pj()
                    mm_group(pbank, pbuf, [(pbank[:, 0:256], xT[:, kc, 128 * tb:128 * tb + 128], wv[:, kc, 0:256], kc == 0, kc == 15) for kc in range(16)],
                             reads=[b_wv, b_xT])
                    vv = Vp[:, 1 + tb, :, :].rearrange("p g (v d) -> p g v d", v=2)
                    pv = pbank[:, 0:256].rearrange("p (g d) -> p g d", d=64)
                    vb = par[:, P_VB:P_VB + 256].rearrange("p (g d) -> p g d", d=64)
                    S.op("dve", lambda e, vv=vv, pv=pv, vb=vb: e.tensor_tensor(out=vv[:, :, 0, 0:64], in0=pv, in1=vb, op=ALU.add),
                         reads=[pbuf] + cb, writes=[bC["Vp"]])
                    S.op("dve", lambda e, vv=vv, pv=pv, vb=vb: e.tensor_tensor(out=vv[:, :, 1, 64:128], in0=pv, in1=vb, op=ALU.add),
                         reads=[pbuf] + cb, writes=[bC["Vp"]])
                S.op("pool", lambda e: e.tensor_copy(out=Kcar[:], in_=KT[:, :, 512:640]), reads=[bC["KT"]], writes=[b_car])
                S.op("pool", lambda e: e.tensor_copy(out=Vcar[:], in_=Vp[:, 4, :, :].rearrange("p g (v d) -> p g v d", v=2)), reads=[bC["Vp"]], writes=[b_car])

                stage('C1')
                blk_i = 0
                for g in range(4):
                    slot, bslot = load_slab(SL_Q + g)
                    for fc in range(4):
                        pbank, pbuf = next_pj()
                        mm_group(pbank, pbuf, [(pbank[:], slot[:, kc, 128 * fc:128 * fc + 128], xT[:, kc, :], kc == 0, kc == 15) for kc in range(16)],
                                 reads=[bslot, b_xT])
                        rope_evac(pbank, pbuf, par[:, P_QB + 4 * g + fc:P_QB + 4 * g + fc + 1], qT[:, fc, :], bC["qT"])
                    slot, bslot = load_slab(SL_G + g)
                    for fc in range(4):
                        pbank, pbuf = next_pj()
                        mm_group(pbank, pbuf, [(pbank[:], slot[:, kc, 128 * fc:128 * fc + 128], xT[:, kc, :], kc == 0, kc == 15) for kc in range(16)],
                                 reads=[bslot, b_xT])
                        S.op("act", lambda e, fc=fc, pbank=pbank: e.activation(out=gT[:, fc, :], in_=pbank[:], func=AF.Silu), reads=[pbuf], writes=[bC["gT"]])
                    for c in range(4):
                        kbs = [0, 1] if not (t == 0 and c == 0) else [1]
                        PTs, bPTs = (PT, bPT) if (blk_i % 2 == 0) else (PT2, bPT2)
                        blk_i += 1
                        for e2 in range(2):
                            for kb in kbs:
                                pS_, bpS_ = ps[3 + (e2 * 2 + kb) % 2], pb[3 + (e2 * 2 + kb) % 2]
                                kcol = 128 * c + 128 * kb
                                mm_group(pS_, bpS_, [(pS_[:].rearrange("p (a b) -> p a b", a=4), KT[64 * e2:64 * e2 + 64, g, kcol:kcol + 128],
                                                      qT[64 * e2:64 * e2 + 64, :, 128 * c:128 * c + 128], True, False),
                                                     (pS_[:], identb[:], (negm4 if kb == 1 else negmp4)[:], False, True)],
                                         reads=[bC["KT"], bC["qT"]] + cb)
                                S.op("act", lambda e, pS_=pS_, dest=PTs[e2][kb]: e.activation(out=dest[:], in_=pS_[:], func=AF.Exp, scale=0.125),
                                     reads=[bpS_], writes=[bPTs[e2][kb]])
                        pO, bpO = ps[5], pb[5]
                        pSm, bpSm = ps[6], pb[6]
                        combos = [(e2, kb) for e2 in range(2) for kb in kbs]
                        mmo, mms_ = [], []
                        for i, (e2, kb) in enumerate(combos):
                            vblk = c + kb
                            mmo.append((pO[:], Vp[:, vblk, g, 128 * e2:128 * e2 + 128], PTs[e2][kb][:], i == 0, i == len(combos) - 1))
                            mms_.append((pSm[:], (oneE if e2 == 0 else oneO)[:], PTs[e2][kb][:], i == 0, i == len(combos) - 1))
                        rds = [bC["Vp"]] + [bPTs[e2][kb] for (e2, kb) in combos] + cb
                        mm_group(pO, bpO, mmo, reads=rds)
                        mm_group(pSm, bpSm, mms_, reads=rds)
                        S.op("dve", lambda e, g=g, pSm=pSm: e.tensor_tensor(out=den[:].rearrange("p (a b) -> p a b", a=4),
                                                                            in0=pSm[:].rearrange("p (a b) -> p a b", a=4),
                                                                            in1=bc(expsink[:, 4 * g:4 * g + 4], 128), op=ALU.add),
                             reads=[bpSm] + cb, writes=[bC["den"]])
                        S.op("dve", lambda e: e.reciprocal(out=den[:], in_=den[:]), reads=[bC["den"]], writes=[bC["den"]])
                        S.op("dve", lambda e, pO=pO: e.tensor_tensor(out=attn[:], in0=pO[:], in1=den[:], op=ALU.mult), reads=[bpO, bC["den"]], writes=[bC["attn"]])
                        S.op("pool", lambda e, g=g, c=c: e.tensor_tensor(out=oT[:, 4 * g:4 * g + 4, 128 * c:128 * c + 128],
                                                                         in0=attn[:].rearrange("p (a b) -> p a b", a=4),
                                                                         in1=gT[:, :, 128 * c:128 * c + 128], op=ALU.mult),
                             reads=[bC["attn"], bC["gT"]], writes=[bC["oT"]])
                stage('C2')
                for cs in range(4):
                    slot, bslot = load_slab(SL_O2 + cs)
                    for tb in range(4):
                        pO, bpO = ps[4 + tb], pb[4 + tb]
                        mm_group(pO, bpO, [(pO[:], oT[:, kc, 128 * tb:128 * tb + 128], slot[:, kc, :], kc == 0, kc == 15) for kc in range(16)],
                                 reads=[bslot, bC["oT"]])
                        S.op("dve", lambda e, tb=tb, cs=cs, pO=pO: e.scalar_tensor_tensor(
                            out=xres[:, tb, 512 * cs:512 * cs + 512], in0=xres[:, tb, 512 * cs:512 * cs + 512], scalar=ALPHA, in1=pO[:],
                            op0=ALU.mult, op1=ALU.add), reads=[bpO, b_xres[tb]], writes=[b_xres[tb]])
                for tb in range(4):
                    layer_norm(tb, 1)
                    r0 = tok0 + 128 * tb
                    S.dma("pool", out_d[r0:r0 + 128, :], xres[:, tb, :], reads=[b_xres[tb]], is_output=True)

        except _Stop:
            pass
        with nc.Block() as block:
            S.emit(block)
    return nc


def _consts():
    c = np.zeros((128, NCST), np.float32)
    c[:, C_ID:C_ID + 128] = np.eye(128)
    i = np.arange(128)
    c[:, C_U:C_U + 128] = (i[:, None] <= i[None, :])
    c[:, C_ONES:C_ONES + 128] = 1.0
    c[:, C_NEGM:C_NEGM + 128] = np.where(i[:, None] <= i[None, :], 0.0, NEG)
    c[:, C_NEGMP:C_NEGMP + 128] = np.where(i[:, None] > i[None, :], 0.0, NEG)
    pm = np.zeros((128, 128), np.float32)
    for f2 in range(128):
        d = f2 % 64
        if d < 8:
            pm[f2 + 8, f2] = 1.0
        elif d < 16:
            pm[f2 - 8, f2] = 1.0
    c[:, C_PMAT:C_PMAT + 128] = pm
    c[:, C_ONE_E:C_ONE_E + 64] = 1.0
    c[:, C_ONE_O + 64:C_ONE_O + 128] = 1.0
    inv = (500000.0 ** (-np.arange(0, 16, 2, dtype=np.float32) / 16)).astype(np.float32)
    for p in range(128):
        d = p % 64
        if d < 16:
            c[p, C_FREQ] = inv[d % 8]
            c[p, C_SGN] = -1.0 if d < 8 else 1.0
    return c


def _params(a_conv_w, a_conv_b, a_dt_bias, a_log, a_d, a_norm_w, kv_b, b_q_bias, b_sinks):
    p = np.zeros((128, NPAR), np.float32)
    cw = a_conv_w[0].reshape(4, 48, 128)
    p[:, P_CW:P_CW + 192] = cw.transpose(2, 1, 0).reshape(128, 192)
    p[:, P_CB:P_CB + 48] = a_conv_b[0].reshape(48, 128).T
    p[:, P_DTB:P_DTB + 64] = a_dt_bias[0][None, :]
    p[:, P_ALOG:P_ALOG + 64] = a_log[0][None, :]
    p[:, P_AD:P_AD + 64] = a_d[0][None, :]
    p[:, P_NW:P_NW + 32] = a_norm_w[0].reshape(32, 128).T
    kb = kv_b[:256].reshape(4, 64)
    p[:, P_KB:P_KB + 4] = np.concatenate([kb, kb], axis=1).T
    p[:, P_VB:P_VB + 256] = kv_b[256:][None, :]
    p[:, P_QB:P_QB + 16] = b_q_bias[0].reshape(16, 128).T
    sk = b_sinks[0].reshape(16, 2)
    p[:64, P_SINK:P_SINK + 16] = sk[:, 0][None, :]
    p[64:, P_SINK:P_SINK + 16] = sk[:, 1][None, :]
    return p


def make_in_maps(inputs, cores):
    f = lambda a: np.ascontiguousarray(np.asarray(a, dtype=np.float32))
    cst = _consts()
    par = _params(*[np.asarray(inputs[k], np.float32) for k in
                    ["a_conv_w", "a_conv_b", "a_dt_bias", "a_log", "a_d", "a_norm_w", "kv_b", "b_q_bias", "b_sinks"]])
    g = np.asarray(inputs["ln_g"], np.float32)
    b = np.asarray(inputs["ln_b"], np.float32)
    lng = np.stack([np.broadcast_to(v[None, :], (128, D)) for v in (g[0], b[0], g[1], b[1])]).astype(np.float32)
    shared = {"w_in": f(inputs["a_w_in"][0]), "w_out": f(inputs["a_w_out"][0]), "kv_w": f(inputs["kv_w"]),
              "bw_in": f(inputs["b_w_in"][0]), "bw_out": f(inputs["b_w_out"][0]), "par": par, "cst": cst,
              "lng": np.ascontiguousarray(lng)}
    maps = []
    for c in cores:
        m = dict(shared)
        m["x"] = f(inputs["x"][c])
        m["pos"] = np.ascontiguousarray(np.asarray(inputs["positions"][c], np.int32).reshape(1, SEQ))
        maps.append(m)
    return maps


def kernel(**inputs):
    nc = build(NT=SEQ // T)
    maps = make_in_maps(inputs, list(range(8)))
    res = run_bass_kernel_spmd(nc, maps, core_ids=list(range(8)))
    return np.stack([r["out"] for r in res.results], axis=0).astype(np.float32)
```
